# Optimizing a Trainium2 kernel written in Bass

```python
import jax, jax.numpy as jnp
from jax import lax
import numpy as np

D_MODEL = 1024
BATCH = 16
SEQ = 2048
DEPTH = 2

GRID_W = 64
CTX_LEN = 256
HEAD_DIM = 64
NA_HEADS = 8
NA_WIDTH = NA_HEADS * HEAD_DIM
KH_MAX = 8
KW = 16
FN_GROUPS = 8
FN_GROUP_DIM = 64
FN_WIDTH = FN_GROUPS * FN_GROUP_DIM
AB_IN = 3 * NA_WIDTH + FN_WIDTH
AB_OUT = NA_WIDTH + FN_WIDTH
CONV_K = 3
PEER_HEADS = 8
PEER_NKEYS = 128
PEER_EXPERTS = PEER_NKEYS * PEER_NKEYS
PEER_DK = 256
PEER_DK_HALF = PEER_DK // 2
PEER_TOPK = 16
PEER_BLOCK = 128
EPS = 1e-6

kernel_name = "hybrid_natten_fnet_shortconv_peer_dit"


def rmsnorm(x, g):
    x32 = x.astype(jnp.float32)
    y = x32 * lax.rsqrt(jnp.mean(x32 * x32, axis=-1, keepdims=True) + EPS)
    return y.astype(x.dtype) * g


def adaln_chunks(cvec, w, b):
    m = jax.nn.silu(cvec) @ w + b
    return jnp.split(m.reshape(-1, 1, m.shape[-1]), 6, axis=-1)


def modulate(x, shift, scale):
    return x * (1 + scale) + shift


def split_heads(t):
    b, n, _ = t.shape
    return t.reshape(b, n, NA_HEADS, HEAD_DIM)


def neighbourhood_attention(q, k, v, k_ctx, v_ctx, rpb):
    b, s, h, dh = q.shape
    rows = s // GRID_W
    kh = min(KH_MAX, rows)
    qg = (q * dh ** -0.5).reshape(b, rows, GRID_W, h, dh)
    kg = k.reshape(b, rows, GRID_W, h, dh)
    vg = v.reshape(b, rows, GRID_W, h, dh)
    cols = np.arange(GRID_W)
    col_start = np.clip(cols - KW // 2, 0, GRID_W - KW)
    in_win = (cols[None, :] >= col_start[:, None]) & (cols[None, :] < col_start[:, None] + KW)
    col_off = np.clip(cols[None, :] - cols[:, None] + KW - 1, 0, 2 * KW - 2)
    rpb32 = rpb.astype(jnp.float32)
    n_win = kh * GRID_W

    def row_step(r):
        start = jnp.clip(r - kh // 2, 0, rows - kh)
        q_r = lax.dynamic_index_in_dim(qg, r, axis=1, keepdims=False)
        k_b = lax.dynamic_slice_in_dim(kg, start, kh, axis=1)
        v_b = lax.dynamic_slice_in_dim(vg, start, kh, axis=1)
        row_off = start + jnp.arange(kh) - r + KH_MAX - 1
        bias = rpb32[:, row_off[:, None, None], col_off[None]]
        bias = jnp.where(in_win[None, None], bias, -jnp.inf).transpose(0, 2, 1, 3)
        s_win = jnp.einsum('bqhd,brkhd->bhqrk', q_r, k_b).astype(jnp.float32) + bias
        s_ctx = jnp.einsum('bqhd,blhd->bhql', q_r, k_ctx).astype(jnp.float32)
        logits = jnp.concatenate([s_win.reshape(b, h, GRID_W, n_win), s_ctx], axis=-1)
        p = jax.nn.softmax(logits, axis=-1).astype(v.dtype)
        p_win = p[..., :n_win].reshape(b, h, GRID_W, kh, GRID_W)
        p_ctx = p[..., n_win:]
        return (jnp.einsum('bhqrk,brkhd->bqhd', p_win, v_b)
                + jnp.einsum('bhql,blhd->bqhd', p_ctx, v_ctx))

    out = lax.map(row_step, jnp.arange(rows))
    return out.transpose(1, 0, 2, 3, 4).reshape(b, s, h * dh)


def context_attention(q, k, v):
    b, n, h, dh = q.shape
    s = jnp.einsum('bqhd,bkhd->bhqk', q * dh ** -0.5, k).astype(jnp.float32)
    p = jax.nn.softmax(s, axis=-1).astype(v.dtype)
    return jnp.einsum('bhqk,bkhd->bqhd', p, v).reshape(b, n, h * dh)


def fourier_mix(f, w):
    b, n, _ = f.shape
    fg = f.reshape(b, n, FN_GROUPS, FN_GROUP_DIM).astype(jnp.float32)
    spec = jnp.fft.fft2(fg, axes=(1, 3), norm="ortho").real.astype(f.dtype)
    return jnp.einsum('bngc,gce->bnge', spec, w).reshape(b, n, FN_WIDTH)


def ab_mixer(h, h_ctx, w_in, w_out, rpb, fn_w, ctx_out):
    q, k, v, f = jnp.split(h @ w_in, [NA_WIDTH, 2 * NA_WIDTH, 3 * NA_WIDTH], axis=-1)
    kc, vc = jnp.split(h_ctx @ w_in[:, NA_WIDTH:3 * NA_WIDTH], 2, axis=-1)
    kc, vc = split_heads(kc), split_heads(vc)
    a = neighbourhood_attention(split_heads(q), split_heads(k), split_heads(v), kc, vc, rpb)
    out = jnp.concatenate([a, fourier_mix(f, fn_w)], axis=-1) @ w_out
    if not ctx_out:
        return out, None
    qc = split_heads(h_ctx @ w_in[:, :NA_WIDTH])
    fc = h_ctx @ w_in[:, 3 * NA_WIDTH:]
    out_c = jnp.concatenate([context_attention(qc, kc, vc), fourier_mix(fc, fn_w)], axis=-1) @ w_out
    return out, out_c


def conv_mixer(h, w_in, conv_w, w_out):
    bg, cg, v = jnp.split(h @ w_in, 3, axis=-1)
    u = jnp.pad(cg * v, ((0, 0), (1, 1), (0, 0)))
    y = conv_w[0] * u[:, :-2] + conv_w[1] * u[:, 1:-1] + conv_w[2] * u[:, 2:]
    return (bg * y) @ w_out


def peer(h, w_q, keys, down, up):
    shape = h.shape
    blocks = h.reshape(-1, PEER_BLOCK, shape[-1])

    def block_fn(hb):
        t = hb.shape[0]
        q = (hb @ w_q).reshape(t, PEER_HEADS, 2, PEER_DK_HALF)
        s = jnp.einsum('thpk,hpnk->thpn', q, keys).astype(jnp.float32)
        vals, idx = lax.top_k(s, PEER_TOPK)
        cand = (vals[:, :, 0, :, None] + vals[:, :, 1, None, :]).reshape(t, PEER_HEADS, PEER_TOPK * PEER_TOPK)
        cand_idx = (idx[:, :, 0, :, None] * PEER_NKEYS + idx[:, :, 1, None, :]).reshape(t, PEER_HEADS, PEER_TOPK * PEER_TOPK)
        top_s, pos = lax.top_k(cand, PEER_TOPK)
        eidx = jnp.take_along_axis(cand_idx, pos, axis=-1)
        g = jax.nn.softmax(top_s, axis=-1)
        act = jax.nn.gelu(jnp.einsum('td,thkd->thk', hb, down[eidx]).astype(jnp.float32), approximate=False)
        w_e = (g * act).astype(hb.dtype)
        return jnp.einsum('thk,thkd->td', w_e, up[eidx])

    return lax.map(block_fn, blocks).reshape(shape)


def setup_inputs(seed: int = 0) -> dict:
    key = jax.random.key(seed)
    ks = jax.random.split(key, 20)
    n_even = (DEPTH + 1) // 2
    n_odd = DEPTH // 2
    D = D_MODEL

    def nrm(k, shape, scale):
        return jax.random.normal(k, shape, jnp.float32) * scale

    return {
        "x": nrm(ks[0], (BATCH, SEQ, D), 1.0),
        "c": nrm(ks[1], (BATCH, D), 1.0),
        "ctx": nrm(ks[2], (BATCH, CTX_LEN, D), 1.0),
        "c_ctx": nrm(ks[3], (D,), 1.0),
        "ada_w": nrm(ks[4], (DEPTH, D, 6 * D), 0.5 * D ** -0.5),
        "ada_b": nrm(ks[5], (DEPTH, 6 * D), 0.02),
        "norm1_g": 1.0 + nrm(ks[6], (DEPTH, D), 0.02),
        "norm2_g": 1.0 + nrm(ks[7], (DEPTH, D), 0.02),
        "final_g": 1.0 + nrm(ks[8], (D,), 0.02),
        "ab_w_in": nrm(ks[9], (n_even, D, AB_IN), D ** -0.5),
        "ab_w_out": nrm(ks[10], (n_even, AB_OUT, D), AB_OUT ** -0.5),
        "na_rpb": nrm(ks[11], (n_even, NA_HEADS, 2 * KH_MAX - 1, 2 * KW - 1), 0.2),
        "fn_w": nrm(ks[12], (n_even, FN_GROUPS, FN_GROUP_DIM, FN_GROUP_DIM), FN_GROUP_DIM ** -0.5),
        "cv_w_in": nrm(ks[13], (n_odd, D, 3 * D), D ** -0.5),
        "cv_w": nrm(ks[14], (n_odd, CONV_K, D), CONV_K ** -0.5),
        "cv_w_out": nrm(ks[15], (n_odd, D, D), D ** -0.5),
        "peer_w_q": nrm(ks[16], (DEPTH, D, PEER_HEADS * PEER_DK), D ** -0.5),
        "peer_keys": nrm(ks[17], (DEPTH, PEER_HEADS, 2, PEER_NKEYS, PEER_DK_HALF), PEER_DK_HALF ** -0.5),
        "peer_down": nrm(ks[18], (DEPTH, PEER_EXPERTS, D), D ** -0.5),
        "peer_up": nrm(ks[19], (DEPTH, PEER_EXPERTS, D), PEER_HEADS ** -0.5),
    }


def reference(x, c, ctx, c_ctx, ada_w, ada_b, norm1_g, norm2_g, final_g,
              ab_w_in, ab_w_out, na_rpb, fn_w, cv_w_in, cv_w, cv_w_out,
              peer_w_q, peer_keys, peer_down, peer_up):
    for i in range(DEPTH):
        even = i % 2 == 0
        upd_ctx = any(j % 2 == 0 for j in range(i + 1, DEPTH))
        sh1, sc1, g1, sh2, sc2, g2 = adaln_chunks(c, ada_w[i], ada_b[i])
        h = modulate(rmsnorm(x, norm1_g[i]), sh1, sc1)
        if even or upd_ctx:
            csh1, csc1, cg1, csh2, csc2, cg2 = adaln_chunks(c_ctx, ada_w[i], ada_b[i])
            hc = modulate(rmsnorm(ctx, norm1_g[i]), csh1, csc1)
        if even:
            e = i // 2
            out, out_c = ab_mixer(h, hc, ab_w_in[e], ab_w_out[e], na_rpb[e], fn_w[e], upd_ctx)
        else:
            o = i // 2
            out = conv_mixer(h, cv_w_in[o], cv_w[o], cv_w_out[o])
            out_c = conv_mixer(hc, cv_w_in[o], cv_w[o], cv_w_out[o]) if upd_ctx else None
        x = x + g1 * out
        x = x + g2 * peer(modulate(rmsnorm(x, norm2_g[i]), sh2, sc2),
                          peer_w_q[i], peer_keys[i], peer_down[i], peer_up[i])
        if upd_ctx:
            ctx = ctx + cg1 * out_c
            ctx = ctx + cg2 * peer(modulate(rmsnorm(ctx, norm2_g[i]), csh2, csc2),
                                   peer_w_q[i], peer_keys[i], peer_down[i], peer_up[i])
    return rmsnorm(x, final_g)
```

```python
from contextlib import ExitStack, contextmanager

import numpy as np
import ml_dtypes
import concourse.bass as bass
import concourse.mybir as mybir
from concourse.bass_utils import run_bass_kernel_spmd

F32 = mybir.dt.float32
BF16 = mybir.dt.bfloat16
U32 = mybir.dt.uint32
AF = mybir.ActivationFunctionType
ALU = mybir.AluOpType
AX = mybir.AxisListType

D = 1024
T = 2048
NB = 2
NCORES = 8
CTX = 256
NEG = -30000.0
EPS = 1e-6


class Ev:
    __slots__ = ("key", "sem", "val")

    def __init__(self, key, sem, val):
        self.key, self.sem, self.val = key, sem, val


class Buf:
    def __init__(self, t=None, name=""):
        self.t = t
        self.name = name
        self.writers = []
        self.readers = []

    def __getitem__(self, idx):
        return self.t[idx]


def _prune(evs):
    best = {}
    pend = []
    for ev in evs:
        if ev.val is None:
            pend.append(ev)
            continue
        o = best.get(ev.key)
        if o is None or o.val < ev.val:
            best[ev.key] = ev
    return list(best.values()) + pend


class Sched:
    def __init__(self, nc, n_dma_sems=40):
        self.nc = nc
        self.engs = dict(pe=nc.tensor, act=nc.scalar, pool=nc.gpsimd, dve=nc.vector, sp=nc.sync)
        self.sem = {}
        self.cnt = {}
        self.seen = {e: {} for e in self.engs}
        self.pending = {e: [] for e in self.engs}
        self.last = {e: None for e in self.engs}
        for e in self.engs:
            self.sem[e] = nc.alloc_semaphore("s_" + e)
            self.cnt[e] = 0
        self.dsem = [nc.alloc_semaphore("d%d" % i) for i in range(n_dma_sems)]
        self.dcnt = [0] * n_dma_sems
        self.dlast = [None] * n_dma_sems
        self.dnext = 0
        self.n_pool = n_dma_sems
        self.qpool = {"sp": (0, 24), "pool": (24, n_dma_sems)}
        self.qnext = {}
        self.uid = 0
        self.own = {}
        self.same_engine_sync = dict(pe=False, act=True, pool=True, dve=True, sp=True)
        self.stacks = [ExitStack()]
        self.n_ops = 0
        self.n_wait = 0

    def sbuf(self, name, shape, dt):
        self.uid += 1
        name = "%s_u%d" % (name, self.uid)
        t = self.stacks[-1].enter_context(self.nc.sbuf_tensor(name, list(shape), dt))
        return Buf(t, name)

    def psum(self, name, shape, dt):
        self.uid += 1
        name = "%s_u%d" % (name, self.uid)
        t = self.stacks[-1].enter_context(self.nc.psum_tensor(name, list(shape), dt))
        return Buf(t, name)

    @contextmanager
    def scope(self):
        st = ExitStack()
        self.stacks.append(st)
        try:
            yield
        finally:
            self.barrier()
            self.stacks.pop()
            st.close()

    def _wait(self, eng, ev):
        if ev.val is None and ev.key == eng:
            return
        if ev.val is None:
            raise RuntimeError("wait on unsignalled event key=%s" % (ev.key,))
        seen = self.seen[eng]
        if seen.get(ev.key, 0) >= ev.val:
            return
        self.engs[eng].wait_ge(ev.sem, ev.val)
        seen[ev.key] = ev.val
        self.n_wait += 1

    def _deps(self, eng, R, W):
        ses = self.same_engine_sync[eng]
        for b in R:
            for ev in b.writers:
                if ev.key == eng and not ses:
                    continue
                self._wait(eng, ev)
        for b in W:
            for ev in b.writers:
                if ev.key == eng and not ses:
                    continue
                self._wait(eng, ev)
            for ev in b.readers:
                if ev.key == eng and not ses:
                    continue
                self._wait(eng, ev)

    def _record(self, ev, R, W):
        for b in R:
            b.readers.append(ev)
            if len(b.readers) > 16:
                b.readers = _prune(b.readers)
        for b in W:
            b.writers = _prune(b.writers + [ev]) if b.writers else [ev]
            b.readers = []

    def op(self, eng, f, R=(), W=(), sig=True):
        self._deps(eng, R, W)
        ins = f(self.engs[eng])
        self.n_ops += 1
        ev = Ev(eng, self.sem[eng], None)
        if sig:
            self.cnt[eng] += 1
            ins.then_inc(self.sem[eng], 1)
            ev.val = self.cnt[eng]
            for p in self.pending[eng]:
                p.val = ev.val
            self.pending[eng] = []
            self.last[eng] = ev
        else:
            self.pending[eng].append(ev)
        self._record(ev, R, W)
        return ev

    def cp(self, eng, out, in_, R=(), W=()):
        if eng == "act":
            return self.op("act", lambda e: e.copy(out=out, in_=in_), R=R, W=W)
        return self.op(eng, lambda e: e.tensor_copy(out=out, in_=in_), R=R, W=W)

    def dma(self, eng, out, in_, R=(), W=(), own_sem=None, **kw):
        if own_sem is not None:
            if own_sem not in self.own:
                self.dsem.append(self.nc.alloc_semaphore("dx%d" % len(self.dsem)))
                self.dcnt.append(0)
                self.dlast.append(None)
                self.own[own_sem] = len(self.dsem) - 1
            i = self.own[own_sem]
        else:
            lo, hi = self.qpool["pool" if eng == "pool" else "sp"]
            i = self.qnext.get(eng, lo)
            self.qnext[eng] = lo + ((i - lo + 1) % (hi - lo))
        prev = self.dlast[i]
        if prev is not None:
            self._wait(eng, prev)
        self._deps(eng, R, W)
        ins = self.engs[eng].dma_start(out=out, in_=in_, **kw)
        self.n_ops += 1
        self.dcnt[i] += 16
        ins.then_inc(self.dsem[i], 16)
        ev = Ev("d%d" % i, self.dsem[i], self.dcnt[i])
        self.dlast[i] = ev
        self._record(ev, R, W)
        return ev

    def barrier(self):
        for e in self.engs:
            assert not self.pending[e], "pending unsignalled ops on %s at barrier" % e
        evs = [ev for ev in self.last.values() if ev is not None]
        evs += [ev for ev in self.dlast if ev is not None]
        for e in self.engs:
            for ev in evs:
                if ev.key != e:
                    self._wait(e, ev)

    def finish(self, bufs, eng="sp"):
        for b in bufs:
            for ev in b.writers:
                self._wait(eng, ev)


def _att_class(j):
    return {0: 0, 1: 1, 14: 3, 15: 4}.get(j, 2)


def _att_kp0(j):
    return min(max(j - 2, 0), 11)


def _rpb_index_table():
    rows, GW, KH, KW = 32, 64, 8, 16
    rep_j = {0: 0, 1: 1, 2: 5, 3: 14, 4: 15}
    idx = np.full((128, 5, 5, 128), 15 * 31, dtype=np.int64)
    cols = np.arange(GW)
    col_start = np.clip(cols - KW // 2, 0, GW - KW)
    for cls in range(5):
        j = rep_j[cls]
        kp0 = _att_kp0(j)
        for p in range(5):
            for kr2 in range(2):
                krow = 2 * (kp0 + p) + kr2
                for qr2 in range(2):
                    qrow = 2 * j + qr2
                    start = min(max(qrow - KH // 2, 0), rows - KH)
                    if not (start <= krow < start + KH):
                        continue
                    ro = krow - qrow + KH - 1
                    for qc in range(GW):
                        kcs = np.arange(col_start[qc], col_start[qc] + KW)
                        co = np.clip(kcs - qc + KW - 1, 0, 2 * KW - 2)
                        idx[kr2 * 64 + kcs, cls, p, qr2 * 64 + qc] = ro * 31 + co
    return idx


_CONST = {}


def _consts():
    if _CONST:
        return _CONST
    n = np.arange(T, dtype=np.int64)
    ang = 2.0 * np.pi * ((n[:, None] * n[None, :]) % T).astype(np.float64) / T
    _CONST["dftc"] = np.cos(ang).astype(ml_dtypes.bfloat16)
    _CONST["dfts"] = (-np.sin(ang)).astype(ml_dtypes.bfloat16)
    c = np.arange(64, dtype=np.int64)
    a2 = 2.0 * np.pi * ((c[:, None] * c[None, :]) % 64).astype(np.float64) / 64
    bdc = np.zeros((128, 128), np.float64)
    bds = np.zeros((128, 128), np.float64)
    for g in range(2):
        bdc[g * 64:(g + 1) * 64, g * 64:(g + 1) * 64] = np.cos(a2)
        bds[g * 64:(g + 1) * 64, g * 64:(g + 1) * 64] = np.sin(a2)
    _CONST["bdc"] = bdc.astype(ml_dtypes.bfloat16)
    _CONST["bds"] = bds.astype(ml_dtypes.bfloat16)
    _CONST["rpb_idx"] = _rpb_index_table()
    return _CONST


def build_nc(nb=NB, stages=("ada", "conv", "l0", "peer0", "l1", "peer1", "final"), peer_groups=8, dbg=False):
    nc = bass.Bass("TRN2", target_bir_lowering=False)
    S = Sched(nc)

    def din(name, shape, dt=F32):
        return nc.dram_tensor(name, list(shape), dt, kind="ExternalInput")

    xT_d = din("xT", [nb, 8, 128, T])
    ctxT_d = din("ctxT", [nb, 8, 128, CTX])
    cT_d = din("cT", [128, 8, nb + 1])
    ada_w_d = din("ada_w", [2, D, 6 * D])
    ada_b_d = din("ada_b", [128, 2, 48])
    n1g_d = din("n1g", [128, 2, 8])
    n2g_d = din("n2g", [128, 2, 8])
    fg_d = din("fg", [128, 8])
    wi_d = din("ab_w_in", [D, 2048])
    wo_d = din("ab_w_out", [D, D])
    biasx_d = din("biasx", [8, 128, 5 * 640])
    fnw_d = din("fn_w", [8, 64, 64])
    dftc_d = din("dftc", [T, T], BF16)
    dfts_d = din("dfts", [T, T], BF16)
    bdc_d = din("bdc", [128, 128], BF16)
    bds_d = din("bds", [128, 128], BF16)
    cwi_d = din("cv_w_in", [D, 3 * D])
    cvw_d = din("cvw", [128, 3, 8])
    cwo_d = din("cv_w_out", [D, D])
    wq_d = din("peer_w_q", [2, D, 2048])
    keysT_d = din("keysT", [2, 16, 128, 128])
    downP_d = din("downP", [2, 128, 128, 1024])
    upP_d = din("upP", [2, 128, 128, 1024])
    outT_d = nc.dram_tensor("outT", [nb, 8, 128, T], F32, kind="ExternalOutput")

    def dscr(name, shape, dt=F32):
        return nc.dram_tensor(name, list(shape), dt, kind="Internal")

    xs_d = [xT_d] + [nc.dram_tensor("xs%d" % i, [nb, 8, 128, T], F32, kind=("ExternalOutput" if dbg else "Internal"))
                     for i in range(1, 5)]
    xs_b = [[[Buf(None, "x%d_%d_%d" % (i, b, g)) for g in range(8)] for b in range(nb)] for i in range(5)]
    downB_d = dscr("downB", [2, 128, 128, 1024], BF16)
    upB_d = dscr("upB", [2, 128, 128, 1024], BF16)
    wqB_d = dscr("wqB", [2, 16, 128, 1024], BF16)
    wq_b = [Buf(None, "wqB%d" % l) for l in range(2)]
    tab_b = [[Buf(None, "tab%d_%d" % (l, q)) for q in range(16)] for l in range(2)]
    out_b = [Buf(None, "out%d" % b) for b in range(nb)]
    dbg_outs = {}

    def dbg_dump(name, src_ap, shape, dt, Rb):
        if not dbg:
            return
        d = nc.dram_tensor("dbg_" + name, list(shape), dt, kind="ExternalOutput")
        b = Buf(None, "dbg_" + name)
        S.dma("sp", d[tuple(slice(None) for _ in shape)], src_ap, R=Rb, W=[b])
        dbg_outs[name] = b

    ident = S.sbuf("ident", [128, 128], F32)
    identb = S.sbuf("identb", [128, 128], BF16)
    ones = S.sbuf("ones", [128, 128], F32)
    iot = S.sbuf("iot", [128, 128], F32)
    pidx = S.sbuf("pidx", [128, 1], F32)
    S.op("pool", lambda e: e.iota(iot[:, :], pattern=[[1, 128]], base=0, channel_multiplier=0,
                                  allow_small_or_imprecise_dtypes=True), W=[iot])
    S.op("pool", lambda e: e.iota(pidx[:, :], pattern=[[0, 1]], base=0, channel_multiplier=1,
                                  allow_small_or_imprecise_dtypes=True), W=[pidx])
    S.op("dve", lambda e: e.tensor_scalar(out=ident[:, :], in0=iot[:, :], scalar1=pidx[:, 0:1], scalar2=None,
                                          op0=ALU.is_equal), R=[iot, pidx], W=[ident])
    S.op("dve", lambda e: e.tensor_copy(out=identb[:, :], in_=ident[:, :]), R=[ident], W=[identb])
    S.op("dve", lambda e: e.memset(ones[:, :], 1.0), W=[ones])
    iotb = S.sbuf("iotb", [128, 128], BF16)
    S.op("dve", lambda e: e.tensor_copy(out=iotb[:, :], in_=iot[:, :]), R=[iot], W=[iotb])

    modT = S.sbuf("modT", [128, 2, 48, nb + 1], F32)
    A1 = S.sbuf("A1", [128, 2, 8, nb + 1], F32)
    A2 = S.sbuf("A2", [128, 2, 8, nb + 1], F32)
    n1g = S.sbuf("n1g_s", [128, 2, 8], F32)
    n2g = S.sbuf("n2g_s", [128, 2, 8], F32)
    fgs = S.sbuf("fg_s", [128, 8], F32)
    zero8 = S.sbuf("zero8", [128, 8], F32)
    cvw = S.sbuf("cvw_s", [128, 3, 8], F32)
    S.dma("sp", n1g[:, :, :], n1g_d[:, :, :], W=[n1g])
    S.dma("sp", n2g[:, :, :], n2g_d[:, :, :], W=[n2g])
    S.dma("sp", fgs[:, :], fg_d[:, :], W=[fgs])
    S.dma("sp", cvw[:, :, :], cvw_d[:, :, :], W=[cvw])
    S.op("dve", lambda e: e.memset(zero8[:, :], 0.0), W=[zero8])

    def conv_wq(l):
        for c in range(16):
            S.dma("pool", wqB_d[l, c].rearrange("p (k n) -> p k n", k=8),
                  wq_d[l, :, c * 128:(c + 1) * 128].rearrange("(k p) n -> p k n", p=128), W=[wq_b[l]], own_sem=32 + c)

    def conv_tab_q(l, q):
        S.dma("pool", downB_d[l, q * 8:(q + 1) * 8], downP_d[l, q * 8:(q + 1) * 8], W=[tab_b[l][q]], own_sem=2 * q)
        S.dma("pool", upB_d[l, q * 8:(q + 1) * 8], upP_d[l, q * 8:(q + 1) * 8], W=[tab_b[l][q]], own_sem=2 * q + 1)

    def conv_tables(l):
        conv_wq(l)
        if l == 0 and "l0" in stages:
            return
        for q in range(16):
            conv_tab_q(l, q)

    def conv0_gen():
        cin = [S.sbuf("cv_in%d" % i, [128, 2, 1024], F32) for i in range(3)]
        cout = [S.sbuf("cv_out%d" % i, [128, 2, 1024], BF16) for i in range(2)]
        tiles = [(src, dst, c0) for c0 in range(0, 128, 2) for (src, dst) in ((downP_d, downB_d), (upP_d, upB_d))]

        def ld(i):
            src, dst, c0 = tiles[i]
            S.dma("sp", cin[i % 3][:, :, :], src[0, c0:c0 + 2].rearrange("c p x -> p c x"), W=[cin[i % 3]])

        ld(0)
        ld(1)
        for i, (src, dst, c0) in enumerate(tiles):
            if i + 2 < len(tiles):
                ld(i + 2)
            ci, co = cin[i % 3], cout[i % 2]
            S.cp("act" if i % 2 else "dve", co[:, :, :], ci[:, :, :], R=[ci], W=[co])
            S.dma("sp", dst[0, c0:c0 + 2].rearrange("c p x -> p c x"), co[:, :, :], R=[co], W=[tab_b[0][c0 // 8]])
            yield

    if "conv" in stages:
        conv_tables(0)

    if "ada" in stages:
        with S.scope():
            ncol = nb + 1
            cTs = S.sbuf("cTs", [128, 8, ncol], F32)
            sil = S.sbuf("sil", [128, 8, ncol], F32)
            adab = S.sbuf("adab", [128, 2, 48], F32)
            S.dma("sp", cTs[:, :, :], cT_d[:, :, :], W=[cTs])
            S.dma("sp", adab[:, :, :], ada_b_d[:, :, :], W=[adab])
            S.op("act", lambda e: e.activation(out=sil[:, :, :], in_=cTs[:, :, :], func=AF.Silu), R=[cTs], W=[sil])
            wts = [S.sbuf("adaw%d" % i, [128, 8, 1024], F32) for i in range(2)]
            pss = [S.psum("adaps%d" % i, [128, 512], F32) for i in range(2)]
            pst = [S.psum("adapt%d" % i, [128, 512], F32) for i in range(2)]
            rows = [S.sbuf("adarow%d" % i, [ncol, 512], F32) for i in range(2)]
            it = 0
            for l in range(2):
                for part in range(6):
                    wt = wts[(l * 6 + part) % 2]
                    S.dma("sp", wt[:, :, :],
                          ada_w_d[l, :, part * 1024:(part + 1) * 1024].rearrange("(k p) n -> p k n", p=128), W=[wt])
                    for hb in range(2):
                        ps, row, pt = pss[it % 2], rows[it % 2], pst[it % 2]
                        it += 1
                        for k in range(8):
                            S.op("pe", lambda e, k=k: e.matmul(ps[0:ncol, :], lhsT=sil[:, k, :],
                                                               rhs=wt[:, k, hb * 512:(hb + 1) * 512],
                                                               start=(k == 0), stop=(k == 7)), R=[wt, sil], W=[ps], sig=(k == 7))
                        S.op("act", lambda e: e.copy(out=row[:, :], in_=ps[0:ncol, :]), R=[ps], W=[row])
                        for q4 in range(4):
                            S.op("pe", lambda e, q4=q4: e.transpose(pt[:, q4 * ncol:(q4 + 1) * ncol],
                                                                    row[:, q4 * 128:(q4 + 1) * 128], ident[0:ncol, 0:ncol]),
                                 R=[row, ident], W=[pt], sig=(q4 == 3))
                        col0 = part * 8 + hb * 4
                        S.op("dve", lambda e, col0=col0, l=l: e.tensor_tensor(
                            out=modT[:, l, col0:col0 + 4, :], in0=pt[:, 0:4 * ncol].rearrange("p (c n) -> p c n", c=4),
                            in1=adab[:, l, col0:col0 + 4].unsqueeze(2).to_broadcast([128, 4, ncol]), op=ALU.add),
                             R=[pt, adab], W=[modT])
            for l in range(2):
                S.op("dve", lambda e, l=l: e.scalar_tensor_tensor(
                    out=A1[:, l, :, :], in0=modT[:, l, 8:16, :], scalar=1.0,
                    in1=n1g[:, l, :].unsqueeze(2).to_broadcast([128, 8, ncol]),
                    op0=ALU.add, op1=ALU.mult), R=[modT, n1g], W=[A1])
                S.op("dve", lambda e, l=l: e.scalar_tensor_tensor(
                    out=A2[:, l, :, :], in0=modT[:, l, 32:40, :], scalar=1.0,
                    in1=n2g[:, l, :].unsqueeze(2).to_broadcast([128, 8, ncol]),
                    op0=ALU.add, op1=ALU.mult), R=[modT, n2g], W=[A2])
            dbg_dump("modT", modT[:, :, :, :], [128, 2, 48, nb + 1], F32, [modT])

    def norm_mod(src_ap_fn, ntok, Rsrc, A_ap, B_ap, dst, dst_is_dram=False, dstW=None, blk=256, tagn="nm"):
        with S.scope():
            xb = [S.sbuf(tagn + "_xb%d" % i, [128, 8, blk], F32) for i in range(2)]
            sq = S.sbuf(tagn + "_sq", [128, 8, blk], F32)
            rs = S.sbuf(tagn + "_rs", [128, blk], F32)
            ps = S.psum(tagn + "_ps", [128, 512], F32)
            ob = [S.sbuf(tagn + "_ob%d" % i, [128, 8, blk], F32) for i in range(2)] if dst_is_dram else None
            for n in range(ntok // blk):
                x_ = xb[n % 2]
                S.dma("sp", x_[:, :, :], src_ap_fn(n * blk, blk), R=Rsrc, W=[x_])
                S.op("act", lambda e, x_=x_: e.activation(out=sq[:, :, :], in_=x_[:, :, :], func=AF.Square),
                     R=[x_], W=[sq])
                for j in range(8):
                    S.op("pe", lambda e, j=j: e.matmul(ps[:, 0:blk], lhsT=ones[:, :], rhs=sq[:, j, :],
                                                       start=(j == 0), stop=(j == 7)),
                         R=[ones, sq], W=[ps], sig=(j == 7))
                S.op("act", lambda e: e.activation(out=rs[:, :], in_=ps[:, 0:blk], func=AF.Sqrt, bias=EPS,
                                                   scale=1.0 / D), R=[ps], W=[rs])
                S.op("dve", lambda e: e.reciprocal(out=rs[:, :], in_=rs[:, :]), R=[rs], W=[rs])
                S.op("dve", lambda e, x_=x_: e.tensor_tensor(
                    out=sq[:, :, :], in0=x_[:, :, :], in1=rs[:, :].unsqueeze(1).to_broadcast([128, 8, blk]),
                    op=ALU.mult), R=[x_, rs], W=[sq])
                S.op("pool", lambda e: e.tensor_tensor(
                    out=sq[:, :, :], in0=sq[:, :, :], in1=A_ap.unsqueeze(2).to_broadcast([128, 8, blk]),
                    op=ALU.mult), R=[sq, A1, A2, fgs], W=[sq])
                if dst_is_dram:
                    o_ = ob[n % 2]
                    S.op("dve", lambda e, o_=o_: e.tensor_tensor(
                        out=o_[:, :, :], in0=sq[:, :, :], in1=B_ap.unsqueeze(2).to_broadcast([128, 8, blk]),
                        op=ALU.add), R=[sq, modT, zero8], W=[o_])
                    S.dma("sp", dst(n * blk, blk), o_[:, :, :], R=[o_], W=dstW)
                else:
                    S.op("dve", lambda e, n=n: e.tensor_tensor(
                        out=dst[:, :, n * blk:(n + 1) * blk], in0=sq[:, :, :],
                        in1=B_ap.unsqueeze(2).to_broadcast([128, 8, blk]),
                        op=ALU.add), R=[sq, modT, zero8], W=[dst])

    def xsrc(i, b):
        return lambda t0, n: xs_d[i][b, :, :, t0:t0 + n].rearrange("j p t -> p j t")

    def grp_bufs(i, b, t0=0, n=T):
        return [xs_b[i][b][g] for g in range(t0 // 256, (t0 + n + 255) // 256)]

    def proj_fm(w_bf, col0, ncols_chunks, hT, ntok, evac, pss, tagc=[0]):
        nblk = (ntok + 511) // 512
        for m in range(ncols_chunks):
            for n in range(nblk):
                w = min(512, ntok - n * 512)
                ps = pss[tagc[0] % len(pss)]
                tagc[0] += 1
                for k in range(8):
                    S.op("pe", lambda e, k=k, m=m, n=n, w=w, ps=ps: e.matmul(
                        ps[:, 0:w], lhsT=w_bf[:, k, col0 + m * 128:col0 + (m + 1) * 128],
                        rhs=hT[:, k, n * 512:n * 512 + w], start=(k == 0), stop=(k == 7)),
                         R=[w_bf, hT], W=[ps], sig=(k == 7))
                evac(m, n, w, ps)

    def out_proj(w_d, srcT, l, b, xin, xout, tagn):
        with S.scope():
            w_bf = S.sbuf(tagn + "_w", [128, 8, D], BF16)
            S.dma("pool", w_bf[:, :, :], w_d.rearrange("(k p) n -> p k n", p=128), W=[w_bf])
            pss = [S.psum(tagn + "_ps%d" % i, [128, 512], F32) for i in range(2)]
            xbs = [S.sbuf(tagn + "_xb%d" % i, [128, 512], F32) for i in range(3)]
            cnt = [0]

            def evac(m, n, w, ps):
                xb = xbs[cnt[0] % 3]
                cnt[0] += 1
                S.dma("sp", xb[:, :], xs_d[xin][b, m, :, n * 512:(n + 1) * 512], R=grp_bufs(xin, b, n * 512, 512), W=[xb])
                S.op("dve", lambda e: e.scalar_tensor_tensor(
                    out=xb[:, :], in0=ps[:, :], scalar=modT[:, l, 16 + m, b:b + 1], in1=xb[:, :],
                    op0=ALU.mult, op1=ALU.add), R=[ps, modT, xb], W=[xb])
                S.dma("sp", xs_d[xout][b, m, :, n * 512:(n + 1) * 512], xb[:, :], R=[xb], W=grp_bufs(xout, b, n * 512, 512))

            proj_fm(w_bf, 0, 8, srcT, T, evac, pss)

    def layer0(b):
        with S.scope():
            catT = S.sbuf("catT", [128, 8, T], BF16)
            with S.scope():
                FT = S.sbuf("FT", [128, 4, T], BF16)
                with S.scope():
                    QT = S.sbuf("QT", [128, 4, T], BF16)
                    KT = S.sbuf("KT", [128, 4, T], BF16)
                    V = S.sbuf("V", [128, 16, 8, 65], BF16)
                    KcT = S.sbuf("KcT", [128, 4, CTX], BF16)
                    Vc = S.sbuf("Vc", [128, 2, 8, 65], BF16)
                    S.op("pool", lambda e: e.memset(V[:, :, :, 64:65], 1.0), W=[V])
                    S.op("pool", lambda e: e.memset(Vc[:, :, :, 64:65], 1.0), W=[Vc])
                    with S.scope():
                        hT = S.sbuf("hT", [128, 8, T], BF16)
                        hcT = S.sbuf("hcT", [128, 8, CTX], BF16)
                        norm_mod(xsrc(0, b), T, [], A1[:, 0, :, b], modT[:, 0, 0:8, b], hT, tagn="n0")
                        norm_mod(lambda t0, n: ctxT_d[b, :, :, t0:t0 + n].rearrange("j p t -> p j t"), CTX, [],
                                 A1[:, 0, :, nb], modT[:, 0, 0:8, nb], hcT, tagn="n0c")
                        if b == 0:
                            dbg_dump("hT", hT[:, :, :], [128, 8, T], BF16, [hT])
                        with S.scope():
                            wi = S.sbuf("wi", [128, 8, 2048], BF16)
                            S.dma("pool", wi[:, :, :], wi_d.rearrange("(k p) n -> p k n", p=128), W=[wi])
                            pss = [S.psum("pj_ps%d" % i, [128, 512], F32) for i in range(3)]
                            ec = [0]

                            def ev_to(dst, scale=None):
                                def evac(m, n, w, ps):
                                    ec[0] += 1
                                    o = dst[:, m, n * 512:n * 512 + w]
                                    if scale is not None:
                                        S.op("act", lambda e: e.activation(out=o, in_=ps[:, 0:w], func=AF.Identity,
                                                                           scale=scale), R=[ps], W=[dst])
                                    elif ec[0] % 2:
                                        S.op("act", lambda e: e.copy(out=o, in_=ps[:, 0:w]), R=[ps], W=[dst])
                                    else:
                                        S.op("dve", lambda e: e.tensor_copy(out=o, in_=ps[:, 0:w]), R=[ps], W=[dst])
                                return evac

                            proj_fm(wi, 0, 4, hT, T, ev_to(QT, 0.125), pss)
                            proj_fm(wi, 512, 4, hT, T, ev_to(KT), pss)
                            proj_fm(wi, 1536, 4, hT, T, ev_to(FT), pss)
                            proj_fm(wi, 512, 4, hcT, CTX, ev_to(KcT), pss)
                            for tt in range(16 + 2):
                                ps = pss[tt % 3]
                                src, c0, dstV, di = (hT, tt * 128, V, tt) if tt < 16 else (hcT, (tt - 16) * 128, Vc, tt - 16)
                                for k in range(8):
                                    S.op("pe", lambda e, k=k, ps=ps, src=src, c0=c0: e.matmul(
                                        ps[:, :], lhsT=src[:, k, c0:c0 + 128], rhs=wi[:, k, 1024:1536],
                                        start=(k == 0), stop=(k == 7)), R=[src, wi], W=[ps], sig=(k == 7))
                                S.cp("act" if tt % 2 else "dve", dstV[:, di, :, 0:64], ps[:, :].rearrange("p (h d) -> p h d", h=8), R=[ps], W=[dstV])
                    if b == 0:
                        dbg_dump("QT", QT[:, :, :], [128, 4, T], BF16, [QT])
                        dbg_dump("V", V[:, :, :, :], [128, 16, 8, 65], BF16, [V])
                    with S.scope():
                        a_tok = S.sbuf("a_tok", [128, 16, 512], BF16)
                        bxs = [S.sbuf("bx%d" % i, [128, 5, 640], F32) for i in range(2)]
                        tmps = [S.sbuf("atmp%d" % i, [128, 640], F32) for i in range(2)]
                        PTs = [S.sbuf("PT%d" % i, [128, 896], BF16) for i in range(2)]
                        rden = [S.sbuf("rden%d" % i, [128, 1], F32) for i in range(2)]
                        ps_s = [[S.psum("ps_s%d_%d" % (i, k), [128, 512], F32) for k in range(2)] for i in range(2)]
                        ps_o = [S.psum("ps_o%d" % i, [128, 512], F32) for i in range(2)]
                        u = 0
                        cgen = conv0_gen() if (b == 0 and "conv" in stages) else None
                        units = [(h, j) for h in range(8) for j in range(16)]
                        bx_of = {}

                        def stageA(u):
                            h, j = units[u]
                            hc, r0 = h // 2, (h % 2) * 64
                            if j == 0:
                                bx_ = bxs[h % 2]
                                S.dma("sp", bx_[:, :, :], biasx_d[h].rearrange("p (c q) -> p c q", c=5), W=[bx_])
                                bx_of[h] = bx_
                            bx = bx_of[h]
                            kp0, cls = _att_kp0(j), _att_class(j)
                            pA, pB = ps_s[u % 2]
                            tmp, PT = tmps[u % 2], PTs[u % 2]
                            q_ap = QT[r0:r0 + 64, hc, j * 128:(j + 1) * 128]
                            for p in range(4):
                                S.op("pe", lambda e, p=p: e.matmul(
                                    pA[:, p * 128:(p + 1) * 128],
                                    lhsT=KT[r0:r0 + 64, hc, (kp0 + p) * 128:(kp0 + p + 1) * 128], rhs=q_ap,
                                    start=True, stop=True), R=[KT, QT], W=[pA], sig=(p == 3))
                            S.op("pe", lambda e: e.matmul(
                                pB[:, 0:128], lhsT=KT[r0:r0 + 64, hc, (kp0 + 4) * 128:(kp0 + 5) * 128], rhs=q_ap,
                                start=True, stop=True), R=[KT, QT], W=[pB], sig=False)
                            for lt in range(2):
                                S.op("pe", lambda e, lt=lt: e.matmul(
                                    pB[:, 128 + lt * 128:256 + lt * 128],
                                    lhsT=KcT[r0:r0 + 64, hc, lt * 128:(lt + 1) * 128], rhs=q_ap,
                                    start=True, stop=True), R=[KcT, QT], W=[pB], sig=(lt == 1))
                            S.op("dve", lambda e: e.tensor_tensor(out=tmp[:, 0:512], in0=pA[:, :], in1=bx[:, cls, 0:512],
                                                                  op=ALU.add), R=[pA, bx], W=[tmp])
                            S.op("dve", lambda e: e.tensor_tensor(out=tmp[:, 512:640], in0=pB[:, 0:128],
                                                                  in1=bx[:, cls, 512:640], op=ALU.add),
                                 R=[pB, bx], W=[tmp])
                            S.op("act", lambda e: e.activation(out=PT[:, 0:640], in_=tmp[:, :], func=AF.Exp),
                                 R=[tmp], W=[PT])
                            S.op("act", lambda e: e.activation(out=PT[:, 640:896], in_=pB[:, 128:384], func=AF.Exp),
                                 R=[pB], W=[PT])

                        def stageB(u):
                            h, j = units[u]
                            kp0 = _att_kp0(j)
                            PT, rd, po = PTs[u % 2], rden[u % 2], ps_o[u % 2]
                            for p in range(7):
                                rhs = V[:, kp0 + p, h, :] if p < 5 else Vc[:, p - 5, h, :]
                                S.op("pe", lambda e, p=p, rhs=rhs: e.matmul(
                                    po[:, 0:65], lhsT=PT[:, p * 128:(p + 1) * 128], rhs=rhs,
                                    start=(p == 0), stop=(p == 6)), R=[PT, V, Vc], W=[po], sig=(p == 6))
                            S.op("dve", lambda e: e.reciprocal(out=rd[:, :], in_=po[:, 64:65]), R=[po], W=[rd])
                            S.op("dve", lambda e: e.tensor_scalar(
                                out=a_tok[:, j, h * 64:(h + 1) * 64], in0=po[:, 0:64], scalar1=rd[:, 0:1],
                                scalar2=None, op0=ALU.mult), R=[po, rd], W=[a_tok])

                        stageA(0)
                        for u in range(len(units)):
                            if u + 1 < len(units):
                                stageA(u + 1)
                            stageB(u)
                            if cgen is not None:
                                next(cgen, None)
                        if cgen is not None:
                            for _ in cgen:
                                pass
                        pts = [S.psum("a_pt%d" % i, [128, 8, 128], BF16) for i in range(2)]
                        for tt in range(16):
                            pt = pts[tt % 2]
                            for m in range(4):
                                S.op("pe", lambda e, m=m: e.transpose(pt[:, m, :], a_tok[:, tt, m * 128:(m + 1) * 128],
                                                                      identb[:, :]),
                                     R=[a_tok, identb], W=[pt], sig=(m == 3))
                            S.cp("act" if tt % 2 else "dve", catT[:, 0:4, tt * 128:(tt + 1) * 128], pt[:, 0:4, :], R=[pt], W=[catT])
                with S.scope():
                    U_sb = S.sbuf("U_sb", [128, 16, 512], BF16)
                    W_sb = S.sbuf("W_sb", [128, 16, 512], BF16)
                    bdc = S.sbuf("bdc", [128, 128], BF16)
                    bds = S.sbuf("bds", [128, 128], BF16)
                    bdw = S.sbuf("bdw", [128, 4, 128], BF16)
                    S.dma("sp", bdc[:, :], bdc_d[:, :], W=[bdc])
                    S.dma("sp", bds[:, :], bds_d[:, :], W=[bds])
                    S.op("dve", lambda e: e.memset(bdw[:, :, :], 0.0), W=[bdw])
                    for g in range(8):
                        r0 = (g % 2) * 64
                        S.dma("pool", bdw[r0:r0 + 64, g // 2, r0:r0 + 64], fnw_d[g], W=[bdw])
                    pss = [S.psum("fn_ps%d" % i, [128, 512], F32) for i in range(4)]
                    for tt in range(16):
                        pu, pw = pss[(tt % 2) * 2], pss[(tt % 2) * 2 + 1]
                        for m in range(4):
                            S.op("pe", lambda e, m=m: e.matmul(pu[:, m * 128:(m + 1) * 128],
                                                               lhsT=FT[:, m, tt * 128:(tt + 1) * 128], rhs=bdc[:, :],
                                                               start=True, stop=True), R=[FT, bdc], W=[pu], sig=(m == 3))
                        for m in range(4):
                            S.op("pe", lambda e, m=m: e.matmul(pw[:, m * 128:(m + 1) * 128],
                                                               lhsT=FT[:, m, tt * 128:(tt + 1) * 128], rhs=bds[:, :],
                                                               start=True, stop=True), R=[FT, bds], W=[pw], sig=(m == 3))
                        S.op("act", lambda e: e.copy(out=U_sb[:, tt, :], in_=pu[:, :]), R=[pu], W=[U_sb])
                        S.op("dve", lambda e: e.tensor_copy(out=W_sb[:, tt, :], in_=pw[:, :]), R=[pw], W=[W_sb])
                    CNs = [S.sbuf("CN%d" % i, [128, 16, 512], BF16) for i in range(2)]
                    SNs = [S.sbuf("SN%d" % i, [128, 16, 512], BF16) for i in range(2)]
                    spT = [S.sbuf("spT%d" % i, [128, 512], BF16) for i in range(2)]
                    it = 0
                    for nbk in range(4):
                        CN, SN = CNs[nbk % 2], SNs[nbk % 2]
                        S.dma("sp", CN[:, :, :], dftc_d[:, nbk * 512:(nbk + 1) * 512].rearrange("(t p) n -> p t n", p=128), W=[CN])
                        S.dma("sp", SN[:, :, :], dfts_d[:, nbk * 512:(nbk + 1) * 512].rearrange("(t p) n -> p t n", p=128), W=[SN])
                        for m in range(4):
                            ps, ps2, sp = pss[it % 2], pss[2 + it % 2], spT[it % 2]
                            it += 1
                            for tt in range(16):
                                S.op("pe", lambda e, tt=tt: e.matmul(ps[:, :], lhsT=U_sb[:, tt, m * 128:(m + 1) * 128],
                                                                     rhs=CN[:, tt, :], start=(tt == 0), stop=False),
                                     R=[U_sb, CN], W=[ps], sig=False)
                                S.op("pe", lambda e, tt=tt: e.matmul(ps[:, :], lhsT=W_sb[:, tt, m * 128:(m + 1) * 128],
                                                                     rhs=SN[:, tt, :], start=False, stop=(tt == 15)),
                                     R=[W_sb, SN], W=[ps], sig=(tt == 15))
                            S.op("act", lambda e: e.activation(out=sp[:, :], in_=ps[:, :], func=AF.Identity,
                                                               scale=float(1.0 / np.sqrt(T * 64.0))), R=[ps], W=[sp])
                            S.op("pe", lambda e: e.matmul(ps2[:, :], lhsT=bdw[:, m, :], rhs=sp[:, :], start=True, stop=True),
                                 R=[bdw, sp], W=[ps2])
                            S.op("dve", lambda e: e.tensor_copy(out=catT[:, 4 + m, nbk * 512:(nbk + 1) * 512], in_=ps2[:, :]),
                                 R=[ps2], W=[catT])
            if b == 0:
                dbg_dump("catT", catT[:, :, :], [128, 8, T], BF16, [catT])
            out_proj(wo_d, catT, 0, b, 0, 1, "o0")

    def layer1(b):
        with S.scope():
            zT = S.sbuf("zT", [128, 8, T], BF16)
            with S.scope():
                hT = S.sbuf("hT1", [128, 8, T], BF16)
                norm_mod(xsrc(2, b), T, grp_bufs(2, b), A1[:, 1, :, b], modT[:, 1, 0:8, b], hT, tagn="n1")
                with S.scope():
                    wci = S.sbuf("wci", [128, 8, 3 * D], BF16)
                    for q in range(3):
                        S.dma("pool", wci[:, :, q * D:(q + 1) * D],
                              cwi_d[:, q * D:(q + 1) * D].rearrange("(k p) n -> p k n", p=128), W=[wci])
                    us = [S.sbuf("u%d" % i, [128, T + 2], F32) for i in range(2)]
                    bgs = [S.sbuf("bg%d" % i, [128, T], F32) for i in range(2)]
                    ys = [S.sbuf("y%d" % i, [128, T], F32) for i in range(2)]
                    tcs = [S.sbuf("tc%d" % i, [128, 512], F32) for i in range(2)]
                    pss = [S.psum("cv_ps%d" % i, [128, 512], F32) for i in range(6)]
                    for i in range(2):
                        S.op("dve", lambda e, i=i: e.memset(us[i][:, :], 0.0), W=[us[i]])
                    cnt = 0
                    for m in range(8):
                        u_, bg, y_ = us[m % 2], bgs[m % 2], ys[m % 2]
                        for n in range(4):
                            pb, pc, pv = pss[(cnt % 2) * 3], pss[(cnt % 2) * 3 + 1], pss[(cnt % 2) * 3 + 2]
                            tc_ = tcs[cnt % 2]
                            cnt += 1
                            for (ps, c0) in ((pb, 0), (pc, D), (pv, 2 * D)):
                                for k in range(8):
                                    S.op("pe", lambda e, k=k, ps=ps, c0=c0: e.matmul(
                                        ps[:, :], lhsT=wci[:, k, c0 + m * 128:c0 + (m + 1) * 128],
                                        rhs=hT[:, k, n * 512:(n + 1) * 512], start=(k == 0), stop=(k == 7)),
                                         R=[wci, hT], W=[ps], sig=(k == 7))
                            S.op("act", lambda e: e.copy(out=bg[:, n * 512:(n + 1) * 512], in_=pb[:, :]), R=[pb], W=[bg])
                            S.op("act", lambda e: e.copy(out=tc_[:, :], in_=pc[:, :]), R=[pc], W=[tc_])
                            S.op("dve", lambda e: e.tensor_tensor(out=u_[:, 1 + n * 512:1 + (n + 1) * 512], in0=tc_[:, :],
                                                                  in1=pv[:, :], op=ALU.mult), R=[tc_, pv], W=[u_])
                        S.op("act", lambda e: e.activation(out=y_[:, :], in_=u_[:, 0:T], func=AF.Copy,
                                                           scale=cvw[:, 0, m:m + 1]), R=[u_, cvw], W=[y_])
                        S.op("dve", lambda e: e.scalar_tensor_tensor(out=y_[:, :], in0=u_[:, 1:T + 1],
                                                                     scalar=cvw[:, 1, m:m + 1], in1=y_[:, :],
                                                                     op0=ALU.mult, op1=ALU.add), R=[u_, cvw, y_], W=[y_])
                        S.op("dve", lambda e: e.scalar_tensor_tensor(out=y_[:, :], in0=u_[:, 2:T + 2],
                                                                     scalar=cvw[:, 2, m:m + 1], in1=y_[:, :],
                                                                     op0=ALU.mult, op1=ALU.add), R=[u_, cvw, y_], W=[y_])
                        S.op("dve", lambda e: e.tensor_tensor(out=zT[:, m, :], in0=y_[:, :], in1=bg[:, :], op=ALU.mult),
                             R=[y_, bg], W=[zT])
            out_proj(cwo_d, zT, 1, b, 2, 3, "o1")

    def peer(l, bs, xin, xout, ngroups):
        items = [(b_, g_) for b_ in bs for g_ in range(ngroups)]
        with S.scope():
            keysT = S.sbuf("keysT", [128, 16, 128], BF16)
            S.dma("pool", keysT[:, :, :], keysT_d[l].rearrange("c k n -> k c n"), W=[keysT])
            iot16 = iot[:, 0:16]
            xgs = [S.sbuf("xg%d" % i, [128, 8, 256], F32) for i in range(2)]
            h2s = [S.sbuf("h2%d" % i, [128, 8, 256], BF16) for i in range(2)]
            TRs = [S.sbuf("TR%d" % i, [128, 2, 3, 128], F32) for i in range(2)]
            scrA = S.sbuf("scrA", [128, 2048], F32)
            scrB = S.sbuf("scrB", [128, 2048], F32)
            rs = S.sbuf("p_rs", [128, 256], F32)
            qT = S.sbuf("qT", [128, 16, 256], BF16)
            wqs = [S.sbuf("wq%d" % i, [128, 8, 128], BF16) for i in range(3)]
            v1 = S.sbuf("v1", [128, 16, 16], F32)
            i1 = S.sbuf("i1", [128, 16, 16], U32)
            i1f = S.sbuf("i1f", [128, 16, 16], F32)
            wks = [S.sbuf("wk%d" % i, [128, 128], F32) for i in range(2)]
            wk2s = [S.sbuf("wk2%d" % i, [128, 256], F32) for i in range(2)]
            ts = S.sbuf("ts", [128, 8, 16], F32)
            pos = S.sbuf("pos", [128, 8, 16], U32)
            k12 = S.sbuf("k12", [128, 2, 128], U32)
            k12f = S.sbuf("k12f", [128, 2, 128], F32)
            ex = S.sbuf("ex", [128, 8, 16], F32)
            zs = S.sbuf("zs", [128, 8], F32)
            TT = S.sbuf("TT", [128, 3, 128], F32)
            NT = 16
            Aohs = [S.sbuf("Aoh%d" % i, [128, NT, 128], BF16) for i in range(2)]
            Bohs = [S.sbuf("Boh%d" % i, [128, NT, 128], BF16) for i in range(2)]
            ohc = [0]
            Aohs_p = [Buf(a.t, "AohP") for a in Aohs]
            Gsb = S.sbuf("Gsb", [128, 256, 128], BF16)
            dns = [S.sbuf("dn%d" % i, [128, 4, 1024], BF16) for i in range(2)]
            ups = [S.sbuf("up%d" % i, [128, 4, 1024], BF16) for i in range(2)]
            gls = [S.sbuf("gl%d" % i, [128, 256], F32) for i in range(2)]
            wEs = [S.sbuf("wE%d" % i, [128, 256], BF16) for i in range(2)]
            acc = [S.psum("acc%d" % i, [128, 512], F32) for i in range(4)]
            pa = [S.psum("pa%d" % i, [128, 512], F32) for i in range(2)]
            bkP = [S.psum("bkP%d" % i, [128, 512], F32) for i in range(2)]
            sq3 = scrA[:, :].rearrange("p (j t) -> p j t", j=8)
            eqt = scrA[:, :].rearrange("p (a i) -> p a i", i=16)
            s_sb = scrB
            cand3 = scrB[:, :].rearrange("p (h c) -> p h c", h=8)
            bkc = [0]
            v1_b = [Buf(v1.t, "v1a"), Buf(v1.t, "v1b")]
            i1_b = [Buf(i1.t, "i1a"), Buf(i1.t, "i1b")]
            ts_b = [Buf(ts.t, "tsa"), Buf(ts.t, "tsb")]
            pos_b = [Buf(pos.t, "posa"), Buf(pos.t, "posb")]

            def nbank():
                bkc[0] += 1
                return bkP[bkc[0] % 2]

            def prep1(it):
                b, g = items[it]
                slot = it % 2
                t0 = g * 256
                xg, h2, TR = xgs[slot], h2s[slot], TRs[slot]
                S.dma("sp", xg[:, :, :], xs_d[xin][b, :, :, t0:t0 + 256].rearrange("j p t -> p j t"),
                      R=[xs_b[xin][b][g]], W=[xg])
                yield
                S.op("act", lambda e: e.activation(out=sq3, in_=xg[:, :, :], func=AF.Square), R=[xg], W=[scrA])
                bn = nbank()
                for j in range(8):
                    S.op("pe", lambda e, j=j: e.matmul(bn[:, 0:256], lhsT=ones[:, :], rhs=sq3[:, j, :],
                                                       start=(j == 0), stop=(j == 7)), R=[ones, scrA], W=[bn], sig=(j == 7))
                yield
                S.op("act", lambda e: e.activation(out=rs[:, :], in_=bn[:, 0:256], func=AF.Sqrt, bias=EPS, scale=1.0 / D),
                     R=[bn], W=[rs])
                S.op("dve", lambda e: e.reciprocal(out=rs[:, :], in_=rs[:, :]), R=[rs], W=[rs])
                S.op("dve", lambda e: e.tensor_tensor(out=sq3, in0=xg[:, :, :],
                                                      in1=rs[:, :].unsqueeze(1).to_broadcast([128, 8, 256]), op=ALU.mult),
                     R=[xg, rs], W=[scrA])
                yield
                yield
                S.op("dve", lambda e: e.tensor_tensor(out=sq3, in0=sq3,
                                                      in1=A2[:, l, :, b].unsqueeze(2).to_broadcast([128, 8, 256]),
                                                      op=ALU.mult), R=[scrA, A2], W=[scrA])
                yield
                S.op("dve", lambda e: e.tensor_tensor(out=h2[:, :, :], in0=sq3,
                                                      in1=modT[:, l, 24:32, b].unsqueeze(2).to_broadcast([128, 8, 256]),
                                                      op=ALU.add), R=[scrA, modT], W=[h2])
                yield
                yield
                yield
                prev = None

                def ldwq(c):
                    S.dma("sp", wqs[c % 3][:, :, :], wqB_d[l, c].rearrange("p (k n) -> p k n", k=8), R=[wq_b[l]],
                          W=[wqs[c % 3]])

                ldwq(0)
                ldwq(1)
                for c in range(16):
                    wq = wqs[c % 3]
                    if c + 2 < 16:
                        ldwq(c + 2)
                    ps = nbank()
                    for k in range(8):
                        S.op("pe", lambda e, k=k: e.matmul(ps[:, 0:256], lhsT=wq[:, k, :], rhs=h2[:, k, :],
                                                           start=(k == 0), stop=(k == 7)), R=[wq, h2], W=[ps], sig=(k == 7))
                    if prev is not None:
                        S.cp("act", qT[:, prev[0], :], prev[1][:, 0:256], R=[prev[1]], W=[qT])
                    prev = (c, ps)
                    yield
                S.cp("act", qT[:, prev[0], :], prev[1][:, 0:256], R=[prev[1]], W=[qT])
                yield
                for tl in range(2):
                    prev = None
                    for q4 in range(4):
                        bq = nbank()
                        for cc in range(4):
                            c = q4 * 4 + cc
                            S.op("pe", lambda e, c=c, cc=cc: e.matmul(bq[:, cc * 128:(cc + 1) * 128],
                                                                      lhsT=qT[:, c, tl * 128:(tl + 1) * 128], rhs=keysT[:, c, :],
                                                                      start=True, stop=True), R=[qT, keysT], W=[bq], sig=(cc == 3))
                        if prev is not None:
                            S.cp("act", s_sb[:, prev[0] * 512:(prev[0] + 1) * 512], prev[1][:, :], R=[prev[1]], W=[scrB])
                        prev = (q4, bq)
                        yield
                    S.cp("act", s_sb[:, prev[0] * 512:(prev[0] + 1) * 512], prev[1][:, :], R=[prev[1]], W=[scrB])
                    yield
                    for c0 in range(0, 16, 2):
                        cs = (c0, c0 + 1)
                        scs = [s_sb[:, c * 128:(c + 1) * 128] for c in cs]
                        for q_, c in enumerate(cs):
                            S.op("dve", lambda e, q_=q_, c=c: e.max(out=v1[:, c, 0:8], in_=scs[q_]), R=[scrB], W=[v1_b[q_]])
                        for q_, c in enumerate(cs):
                            S.op("dve", lambda e, q_=q_, c=c: e.max_index(out=i1[:, c, 0:8], in_max=v1[:, c, 0:8],
                                                                         in_values=scs[q_]), R=[scrB, v1_b[q_]], W=[i1_b[q_]])
                        for q_, c in enumerate(cs):
                            S.op("dve", lambda e, q_=q_, c=c: e.match_replace(out=wks[q_][:, :], in_to_replace=v1[:, c, 0:8],
                                                                             in_values=scs[q_], imm_value=-1e30),
                                 R=[scrB, v1_b[q_]], W=[wks[q_]])
                        yield
                        for q_, c in enumerate(cs):
                            S.op("dve", lambda e, q_=q_, c=c: e.max(out=v1[:, c, 8:16], in_=wks[q_][:, :]), R=[wks[q_]], W=[v1_b[q_]])
                        for q_, c in enumerate(cs):
                            S.op("dve", lambda e, q_=q_, c=c: e.max_index(out=i1[:, c, 8:16], in_max=v1[:, c, 8:16],
                                                                         in_values=wks[q_][:, :]), R=[wks[q_], v1_b[q_]], W=[i1_b[q_]])
                        yield
                    v1v = v1[:, :, :].rearrange("p (h two) k -> p h two k", two=2)
                    S.op("dve", lambda e: e.tensor_tensor(
                        out=cand3.rearrange("p h (i j) -> p h i j", i=16),
                        in0=v1v[:, :, 0, :].unsqueeze(3).to_broadcast([128, 8, 16, 16]),
                        in1=v1v[:, :, 1, :].unsqueeze(2).to_broadcast([128, 8, 16, 16]), op=ALU.add), R=v1_b, W=[scrB])
                    yield
                    for h0 in range(0, 8, 2):
                        hs = (h0, h0 + 1)
                        chs = [cand3[:, h, :] for h in hs]
                        for q_, h in enumerate(hs):
                            S.op("dve", lambda e, q_=q_, h=h: e.max(out=ts[:, h, 0:8], in_=chs[q_]), R=[scrB], W=[ts_b[q_]])
                        for q_, h in enumerate(hs):
                            S.op("dve", lambda e, q_=q_, h=h: e.max_index(out=pos[:, h, 0:8], in_max=ts[:, h, 0:8],
                                                                         in_values=chs[q_]), R=[scrB, ts_b[q_]], W=[pos_b[q_]])
                        for q_, h in enumerate(hs):
                            S.op("dve", lambda e, q_=q_, h=h: e.match_replace(out=wk2s[q_][:, :], in_to_replace=ts[:, h, 0:8],
                                                                             in_values=chs[q_], imm_value=-1e30),
                                 R=[scrB, ts_b[q_]], W=[wk2s[q_]])
                        yield
                        for q_, h in enumerate(hs):
                            S.op("dve", lambda e, q_=q_, h=h: e.max(out=ts[:, h, 8:16], in_=wk2s[q_][:, :]), R=[wk2s[q_]], W=[ts_b[q_]])
                        for q_, h in enumerate(hs):
                            S.op("dve", lambda e, q_=q_, h=h: e.max_index(out=pos[:, h, 8:16], in_max=ts[:, h, 8:16],
                                                                         in_values=wk2s[q_][:, :]), R=[wk2s[q_], ts_b[q_]], W=[pos_b[q_]])
                        yield
                    posf = pos[:, :, :].rearrange("p h k -> p (h k)")
                    S.op("dve", lambda e: e.tensor_single_scalar(out=k12[:, 0, :], in_=posf, scalar=4,
                                                                 op=ALU.logical_shift_right), R=pos_b, W=[k12])
                    S.op("dve", lambda e: e.tensor_single_scalar(out=k12[:, 1, :], in_=posf, scalar=15,
                                                                 op=ALU.bitwise_and), R=pos_b, W=[k12])
                    S.op("dve", lambda e: e.tensor_copy(out=k12f[:, :, :], in_=k12[:, :, :]), R=[k12], W=[k12f])
                    S.op("dve", lambda e: e.tensor_copy(out=i1f[:, :, :], in_=i1[:, :, :]), R=i1_b, W=[i1f])
                    yield
                    i1v = i1f[:, :, :].rearrange("p (h two) k -> p h two k", two=2)
                    eq4 = scrA[:, :].rearrange("p (h k i) -> p h k i", h=8, k=16)
                    for w_ in range(2):
                        S.op("dve", lambda e, w_=w_: e.tensor_tensor(
                            out=eqt, in0=k12f[:, w_, :].unsqueeze(2).to_broadcast([128, 128, 16]),
                            in1=iot16.unsqueeze(1).to_broadcast([128, 128, 16]), op=ALU.is_equal), R=[k12f, iot], W=[scrA])
                        S.op("dve", lambda e, w_=w_: e.tensor_tensor(
                            out=eq4, in0=eq4, in1=i1v[:, :, w_, :].unsqueeze(2).to_broadcast([128, 8, 16, 16]),
                            op=ALU.mult), R=[scrA, i1f], W=[scrA])
                        S.op("dve", lambda e, w_=w_: e.tensor_reduce(out=TR[:, tl, w_, :], in_=eqt, axis=AX.X, op=ALU.add),
                             R=[scrA], W=[TR])
                        yield
                    S.op("dve", lambda e: e.tensor_tensor(out=ex[:, :, :], in0=ts[:, :, :],
                                                          in1=ts[:, :, 0:1].to_broadcast([128, 8, 16]), op=ALU.subtract),
                         R=ts_b, W=[ex])
                    yield
                    yield
                    S.op("act", lambda e: e.activation(out=ex[:, :, :], in_=ex[:, :, :], func=AF.Exp), R=[ex], W=[ex])
                    yield
                    S.op("dve", lambda e: e.tensor_reduce(out=zs[:, :], in_=ex[:, :, :], axis=AX.X, op=ALU.add), R=[ex], W=[zs])
                    S.op("dve", lambda e: e.reciprocal(out=zs[:, :], in_=zs[:, :]), R=[zs], W=[zs])
                    S.op("dve", lambda e: e.tensor_tensor(
                        out=TR[:, tl, 2, :].rearrange("p (h k) -> p h k", h=8), in0=ex[:, :, :],
                        in1=zs[:, :].unsqueeze(2).to_broadcast([128, 8, 16]), op=ALU.mult), R=[ex, zs], W=[TR])
                    yield

            def prep2(it):
                slot = it % 2
                TR = TRs[slot]
                for tl in range(2):
                    bt = nbank()
                    for w_ in range(3):
                        S.op("pe", lambda e, w_=w_: e.transpose(bt[:, w_ * 128:(w_ + 1) * 128], TR[:, tl, w_, :], ident[:, :]),
                             R=[TR, ident], W=[bt], sig=(w_ == 2))
                    S.op("act", lambda e: e.copy(out=TT[:, :, :].rearrange("p a t -> p (a t)"), in_=bt[:, 0:384]),
                         R=[bt], W=[TT])
                    for hf in range(128 // NT):
                        tsl = slice(hf * NT, (hf + 1) * NT)
                        Aoh, Boh, AohP = Aohs[ohc[0] % 2], Bohs[ohc[0] % 2], Aohs_p[ohc[0] % 2]
                        ohc[0] += 1
                        for t_ in range(NT):
                            col = hf * NT + t_
                            S.op("dve", lambda e, t_=t_, col=col: e.tensor_scalar(
                                out=Aoh[:, t_, :], in0=iotb[:, :], scalar1=TT[:, 0, col:col + 1], scalar2=TT[:, 2, col:col + 1],
                                op0=ALU.is_equal, op1=ALU.mult), R=[iotb, TT], W=[Aoh], sig=False)
                            S.op("dve", lambda e, t_=t_, col=col: e.tensor_scalar(
                                out=Boh[:, t_, :], in0=iotb[:, :], scalar1=TT[:, 1, col:col + 1], scalar2=None,
                                op0=ALU.is_equal), R=[iotb, TT], W=[Boh], sig=(t_ == NT - 1))
                        for t4 in range(NT // 4):
                            pg = nbank()
                            for tq in range(4):
                                t_ = t4 * 4 + tq
                                S.op("pe", lambda e, t_=t_, tq=tq: e.matmul(
                                    pg[:, tq * 128:(tq + 1) * 128], lhsT=Aoh[:, t_, :], rhs=Boh[:, t_, :],
                                    start=True, stop=True), R=[Aoh, Boh], W=[pg], sig=(tq == 3))
                            tg = tl * 128 + hf * NT + t4 * 4
                            S.cp("act", Gsb[:, tg:tg + 4, :].rearrange("p t i -> p (t i)"), pg[:, :], R=[pg], W=[Gsb])

            def loop(it, nxt):
                b, g = items[it]
                slot = it % 2
                t0 = g * 256
                xg, h2 = xgs[slot], h2s[slot]

                def load(k):
                    S.dma("sp", dns[k % 2][:, :, :], downB_d[l, k * 4:(k + 1) * 4].rearrange("c p x -> p c x"),
                          R=[tab_b[l][k // 2]], W=[dns[k % 2]])
                    S.dma("sp", ups[k % 2][:, :, :], upB_d[l, k * 4:(k + 1) * 4].rearrange("c p x -> p c x"),
                          R=[tab_b[l][k // 2]], W=[ups[k % 2]])

                def down(c):
                    dn, ci = dns[(c // 4) % 2], c % 4
                    p_, gl, wE = pa[c % 2], gls[c % 2], wEs[c % 2]
                    for dk in range(8):
                        S.op("pe", lambda e, dk=dk: e.matmul(p_[:, 0:256], lhsT=dn[:, ci, dk * 128:(dk + 1) * 128],
                                                             rhs=h2[:, dk, :], start=(dk == 0), stop=(dk == 7)),
                             R=[dn, h2], W=[p_], sig=(dk == 7))
                    S.op("act", lambda e: e.activation(out=gl[:, :], in_=p_[:, 0:256], func=AF.Gelu), R=[p_], W=[gl])
                    S.op("dve", lambda e: e.tensor_tensor(out=wE[:, :], in0=gl[:, :], in1=Gsb[:, :, c], op=ALU.mult),
                         R=[gl, Gsb], W=[wE])

                def up_(c):
                    up, ci, wE = ups[(c // 4) % 2], c % 4, wEs[c % 2]
                    for dk in range(8):
                        a_ = acc[dk // 2]
                        S.op("pe", lambda e, dk=dk, a_=a_: e.matmul(
                            a_[:, (dk % 2) * 256:(dk % 2 + 1) * 256], lhsT=up[:, ci, dk * 128:(dk + 1) * 128], rhs=wE[:, :],
                            start=(c == 0 and dk % 2 == 0), stop=(c == 127 and dk % 2 == 1)), R=[up, wE], W=[a_],
                             sig=(dk == 7))

                load(0)
                load(1)
                down(0)
                for c in range(128):
                    if c + 1 < 128:
                        down(c + 1)
                    up_(c)
                    if c % 4 == 3 and c // 4 + 2 < 32:
                        load(c // 4 + 2)
                    if nxt is not None:
                        next(nxt, None)
                for dk in range(8):
                    a_ = acc[dk // 2]
                    S.op("dve", lambda e, dk=dk, a_=a_: e.scalar_tensor_tensor(
                        out=xg[:, dk, :], in0=a_[:, (dk % 2) * 256:(dk % 2 + 1) * 256],
                        scalar=modT[:, l, 40 + dk, b:b + 1], in1=xg[:, dk, :], op0=ALU.mult, op1=ALU.add),
                         R=[a_, modT, xg], W=[xg])
                S.dma("sp", xs_d[xout][b, :, :, t0:t0 + 256].rearrange("j p t -> p j t"), xg[:, :, :],
                      R=[xg], W=[xs_b[xout][b][g]])

            for _ in prep1(0):
                pass
            for it in range(len(items)):
                prep2(it)
                if l == 0 and "conv" in stages and it < 8:
                    if it == 0:
                        conv_wq(1)
                    for q in range(2 * it, 2 * it + 2):
                        conv_tab_q(1, q)
                nxt = prep1(it + 1) if it + 1 < len(items) else None
                loop(it, nxt)
                if nxt is not None:
                    for _ in nxt:
                        pass

    bs = list(range(nb))
    if "l0" in stages:
        for b in bs:
            layer0(b)
    if "conv" in stages and "peer0" not in stages:
        conv_tables(1)
    if "peer0" in stages:
        peer(0, bs, 1, 2, peer_groups)
    if "l1" in stages:
        for b in bs:
            layer1(b)
    if "peer1" in stages:
        peer(1, bs, 3, 4, peer_groups)
    if "final" in stages:
        for b in bs:
            norm_mod(xsrc(4, b), T, grp_bufs(4, b), fgs[:, :], zero8[:, :],
                     lambda t0, n, b=b: outT_d[b, :, :, t0:t0 + n].rearrange("j p t -> p j t"),
                     dst_is_dram=True, dstW=[out_b[b]], tagn="nf")
    S.finish(out_b + list(dbg_outs.values()) + [xb_ for i in range(5) for b_ in xs_b[i] for xb_ in b_], "sp")
    S.barrier()
    S.stacks[0].close()
    print("built: ops=%d waits=%d" % (S.n_ops, S.n_wait))
    return nc


def _fm(a):
    B, Tn, Dn = a.shape
    return np.ascontiguousarray(a.transpose(0, 2, 1).reshape(B, Dn // 128, 128, Tn))


def _pc(v):
    v = np.asarray(v)
    lead = v.shape[:-1]
    return np.ascontiguousarray(np.moveaxis(v.reshape(lead + (8, 128)), -1, 0))


def host_layout(inp, nb=NB, ncores=NCORES):
    cst = _consts()
    f32 = np.float32
    shared = {}
    shared["ada_w"] = np.ascontiguousarray(inp["ada_w"], f32)
    shared["ada_b"] = np.ascontiguousarray(inp["ada_b"].reshape(2, 48, 128).transpose(2, 0, 1), f32)
    shared["n1g"] = _pc(inp["norm1_g"]).astype(f32)
    shared["n2g"] = _pc(inp["norm2_g"]).astype(f32)
    shared["fg"] = _pc(inp["final_g"]).astype(f32)
    shared["ab_w_in"] = np.ascontiguousarray(inp["ab_w_in"][0], f32)
    shared["ab_w_out"] = np.ascontiguousarray(inp["ab_w_out"][0], f32)
    rpb = np.asarray(inp["na_rpb"][0], f32).reshape(8, 15 * 31)
    rpb_ext = np.concatenate([rpb, np.full((8, 1), NEG, f32)], axis=1)
    shared["biasx"] = np.ascontiguousarray(rpb_ext[:, cst["rpb_idx"]].reshape(8, 128, 5 * 640))
    shared["fn_w"] = np.ascontiguousarray(inp["fn_w"][0], f32)
    shared["dftc"] = cst["dftc"]
    shared["dfts"] = cst["dfts"]
    shared["bdc"] = cst["bdc"]
    shared["bds"] = cst["bds"]
    shared["cv_w_in"] = np.ascontiguousarray(inp["cv_w_in"][0], f32)
    shared["cvw"] = _pc(inp["cv_w"][0]).astype(f32)
    shared["cv_w_out"] = np.ascontiguousarray(inp["cv_w_out"][0], f32)
    shared["peer_w_q"] = np.ascontiguousarray(inp["peer_w_q"], f32)
    shared["keysT"] = np.ascontiguousarray(np.asarray(inp["peer_keys"], f32).reshape(2, 16, 128, 128).transpose(0, 1, 3, 2))
    dn = np.asarray(inp["peer_down"], f32).reshape(2, 128, 128, 8, 128)
    shared["downP"] = np.ascontiguousarray(dn.transpose(0, 2, 4, 3, 1)).reshape(2, 128, 128, 1024)
    up = np.asarray(inp["peer_up"], f32).reshape(2, 128, 128, 1024)
    shared["upP"] = np.ascontiguousarray(up.transpose(0, 2, 1, 3))
    xT = _fm(np.asarray(inp["x"], f32))
    ctxT = _fm(np.asarray(inp["ctx"], f32))
    c = np.asarray(inp["c"], f32)
    cc = np.asarray(inp["c_ctx"], f32)
    maps = []
    for i in range(ncores):
        m = dict(shared)
        m["xT"] = xT[i * nb:(i + 1) * nb]
        m["ctxT"] = ctxT[i * nb:(i + 1) * nb]
        cols = np.stack([c[i * nb + k] for k in range(nb)] + [cc], axis=-1)
        m["cT"] = np.ascontiguousarray(cols.reshape(8, 128, nb + 1).transpose(1, 0, 2))
        maps.append(m)
    return maps


def kernel(**inputs):
    maps = host_layout(inputs)
    nc = build_nc()
    res = run_bass_kernel_spmd(nc, maps, core_ids=list(range(NCORES)))
    outs = []
    for r in res.results:
        o = np.asarray(r["outT"])
        outs.append(o.reshape(NB, D, T).transpose(0, 2, 1))
    return np.ascontiguousarray(np.concatenate(outs, axis=0), dtype=np.float32)
```

```python
from contextlib import ExitStack, contextmanager

import numpy as np
import ml_dtypes
import concourse.bass as bass
import concourse.mybir as mybir
from concourse.bass_utils import run_bass_kernel_spmd

F32 = mybir.dt.float32
BF16 = mybir.dt.bfloat16
U32 = mybir.dt.uint32
AF = mybir.ActivationFunctionType
ALU = mybir.AluOpType
AX = mybir.AxisListType

D = 1024
T = 2048
NB = 2
NCORES = 8
CTX = 256
NEG = -30000.0
EPS = 1e-6


class Ev:
    __slots__ = ("key", "sem", "val")

    def __init__(self, key, sem, val):
        self.key, self.sem, self.val = key, sem, val


class Buf:
    def __init__(self, t=None, name=""):
        self.t = t
        self.name = name
        self.writers = []
        self.readers = []

    def __getitem__(self, idx):
        return self.t[idx]


def _prune(evs):
    best = {}
    pend = []
    for ev in evs:
        if ev.val is None:
            pend.append(ev)
            continue
        o = best.get(ev.key)
        if o is None or o.val < ev.val:
            best[ev.key] = ev
    return list(best.values()) + pend


class Sched:
    def __init__(self, nc, n_dma_sems=40):
        self.nc = nc
        self.engs = dict(pe=nc.tensor, act=nc.scalar, pool=nc.gpsimd, dve=nc.vector, sp=nc.sync)
        self.sem = {}
        self.cnt = {}
        self.seen = {e: {} for e in self.engs}
        self.pending = {e: [] for e in self.engs}
        self.last = {e: None for e in self.engs}
        for e in self.engs:
            self.sem[e] = nc.alloc_semaphore("s_" + e)
            self.cnt[e] = 0
        self.dsem = [nc.alloc_semaphore("d%d" % i) for i in range(n_dma_sems)]
        self.dcnt = [0] * n_dma_sems
        self.dlast = [None] * n_dma_sems
        self.dnext = 0
        self.n_pool = n_dma_sems
        self.qpool = {"sp": (0, 24), "pool": (24, n_dma_sems)}
        self.qnext = {}
        self.uid = 0
        self.own = {}
        self.same_engine_sync = dict(pe=False, act=True, pool=True, dve=True, sp=True)
        self.stacks = [ExitStack()]
        self.n_ops = 0
        self.n_wait = 0

    def sbuf(self, name, shape, dt):
        self.uid += 1
        name = "%s_u%d" % (name, self.uid)
        t = self.stacks[-1].enter_context(self.nc.sbuf_tensor(name, list(shape), dt))
        return Buf(t, name)

    def psum(self, name, shape, dt):
        self.uid += 1
        name = "%s_u%d" % (name, self.uid)
        t = self.stacks[-1].enter_context(self.nc.psum_tensor(name, list(shape), dt))
        return Buf(t, name)

    @contextmanager
    def scope(self):
        st = ExitStack()
        self.stacks.append(st)
        try:
            yield
        finally:
            self.barrier()
            self.stacks.pop()
            st.close()

    def _wait(self, eng, ev):
        if ev.val is None and ev.key == eng:
            return
        if ev.val is None:
            raise RuntimeError("wait on unsignalled event key=%s" % (ev.key,))
        seen = self.seen[eng]
        if seen.get(ev.key, 0) >= ev.val:
            return
        self.engs[eng].wait_ge(ev.sem, ev.val)
        seen[ev.key] = ev.val
        self.n_wait += 1

    def _deps(self, eng, R, W):
        ses = self.same_engine_sync[eng]
        for b in R:
            for ev in b.writers:
                if ev.key == eng and not ses:
                    continue
                self._wait(eng, ev)
        for b in W:
            for ev in b.writers:
                if ev.key == eng and not ses:
                    continue
                self._wait(eng, ev)
            for ev in b.readers:
                if ev.key == eng and not ses:
                    continue
                self._wait(eng, ev)

    def _record(self, ev, R, W):
        for b in R:
            b.readers.append(ev)
            if len(b.readers) > 16:
                b.readers = _prune(b.readers)
        for b in W:
            b.writers = _prune(b.writers + [ev]) if b.writers else [ev]
            b.readers = []

    def op(self, eng, f, R=(), W=(), sig=True):
        self._deps(eng, R, W)
        ins = f(self.engs[eng])
        self.n_ops += 1
        ev = Ev(eng, self.sem[eng], None)
        if sig:
            self.cnt[eng] += 1
            ins.then_inc(self.sem[eng], 1)
            ev.val = self.cnt[eng]
            for p in self.pending[eng]:
                p.val = ev.val
            self.pending[eng] = []
            self.last[eng] = ev
        else:
            self.pending[eng].append(ev)
        self._record(ev, R, W)
        return ev

    def cp(self, eng, out, in_, R=(), W=()):
        if eng == "act":
            return self.op("act", lambda e: e.copy(out=out, in_=in_), R=R, W=W)
        return self.op(eng, lambda e: e.tensor_copy(out=out, in_=in_), R=R, W=W)

    def dma(self, eng, out, in_, R=(), W=(), own_sem=None, **kw):
        if own_sem is not None:
            if own_sem not in self.own:
                self.dsem.append(self.nc.alloc_semaphore("dx%d" % len(self.dsem)))
                self.dcnt.append(0)
                self.dlast.append(None)
                self.own[own_sem] = len(self.dsem) - 1
            i = self.own[own_sem]
        else:
            lo, hi = self.qpool["pool" if eng == "pool" else "sp"]
            i = self.qnext.get(eng, lo)
            self.qnext[eng] = lo + ((i - lo + 1) % (hi - lo))
        prev = self.dlast[i]
        if prev is not None:
            self._wait(eng, prev)
        self._deps(eng, R, W)
        ins = self.engs[eng].dma_start(out=out, in_=in_, **kw)
        self.n_ops += 1
        self.dcnt[i] += 16
        ins.then_inc(self.dsem[i], 16)
        ev = Ev("d%d" % i, self.dsem[i], self.dcnt[i])
        self.dlast[i] = ev
        self._record(ev, R, W)
        return ev

    def barrier(self):
        for e in self.engs:
            assert not self.pending[e], "pending unsignalled ops on %s at barrier" % e
        evs = [ev for ev in self.last.values() if ev is not None]
        evs += [ev for ev in self.dlast if ev is not None]
        for e in self.engs:
            for ev in evs:
                if ev.key != e:
                    self._wait(e, ev)

    def finish(self, bufs, eng="sp"):
        for b in bufs:
            for ev in b.writers:
                self._wait(eng, ev)


def _att_class(j):
    return {0: 0, 1: 1, 14: 3, 15: 4}.get(j, 2)


def _att_kp0(j):
    return min(max(j - 2, 0), 11)


def _rpb_index_table():
    rows, GW, KH, KW = 32, 64, 8, 16
    rep_j = {0: 0, 1: 1, 2: 5, 3: 14, 4: 15}
    idx = np.full((128, 5, 5, 128), 15 * 31, dtype=np.int64)
    cols = np.arange(GW)
    col_start = np.clip(cols - KW // 2, 0, GW - KW)
    for cls in range(5):
        j = rep_j[cls]
        kp0 = _att_kp0(j)
        for p in range(5):
            for kr2 in range(2):
                krow = 2 * (kp0 + p) + kr2
                for qr2 in range(2):
                    qrow = 2 * j + qr2
                    start = min(max(qrow - KH // 2, 0), rows - KH)
                    if not (start <= krow < start + KH):
                        continue
                    ro = krow - qrow + KH - 1
                    for qc in range(GW):
                        kcs = np.arange(col_start[qc], col_start[qc] + KW)
                        co = np.clip(kcs - qc + KW - 1, 0, 2 * KW - 2)
                        idx[kr2 * 64 + kcs, cls, p, qr2 * 64 + qc] = ro * 31 + co
    return idx


_CONST = {}


def _consts():
    if _CONST:
        return _CONST
    n = np.arange(T, dtype=np.int64)
    ang = 2.0 * np.pi * ((n[:, None] * n[None, :]) % T).astype(np.float64) / T
    _CONST["dftc"] = np.cos(ang).astype(ml_dtypes.bfloat16)
    _CONST["dfts"] = (-np.sin(ang)).astype(ml_dtypes.bfloat16)
    c = np.arange(64, dtype=np.int64)
    a2 = 2.0 * np.pi * ((c[:, None] * c[None, :]) % 64).astype(np.float64) / 64
    bdc = np.zeros((128, 128), np.float64)
    bds = np.zeros((128, 128), np.float64)
    for g in range(2):
        bdc[g * 64:(g + 1) * 64, g * 64:(g + 1) * 64] = np.cos(a2)
        bds[g * 64:(g + 1) * 64, g * 64:(g + 1) * 64] = np.sin(a2)
    _CONST["bdc"] = bdc.astype(ml_dtypes.bfloat16)
    _CONST["bds"] = bds.astype(ml_dtypes.bfloat16)
    _CONST["rpb_idx"] = _rpb_index_table()
    return _CONST


def build_nc(nb=NB, stages=("ada", "conv", "l0", "peer0", "l1", "peer1", "final"), peer_groups=8, dbg=False):
    nc = bass.Bass("TRN2", target_bir_lowering=False)
    S = Sched(nc)

    def din(name, shape, dt=F32):
        return nc.dram_tensor(name, list(shape), dt, kind="ExternalInput")

    xT_d = din("xT", [nb, 8, 128, T])
    ctxT_d = din("ctxT", [nb, 8, 128, CTX])
    cT_d = din("cT", [128, 8, nb + 1])
    ada_w_d = din("ada_w", [2, D, 6 * D])
    ada_b_d = din("ada_b", [128, 2, 48])
    n1g_d = din("n1g", [128, 2, 8])
    n2g_d = din("n2g", [128, 2, 8])
    fg_d = din("fg", [128, 8])
    wi_d = din("ab_w_in", [D, 2048])
    wo_d = din("ab_w_out", [D, D])
    biasx_d = din("biasx", [8, 128, 5 * 640])
    fnw_d = din("fn_w", [8, 64, 64])
    dftc_d = din("dftc", [T, T], BF16)
    dfts_d = din("dfts", [T, T], BF16)
    bdc_d = din("bdc", [128, 128], BF16)
    bds_d = din("bds", [128, 128], BF16)
    cwi_d = din("cv_w_in", [D, 3 * D])
    cvw_d = din("cvw", [128, 3, 8])
    cwo_d = din("cv_w_out", [D, D])
    wq_d = din("peer_w_q", [2, D, 2048])
    keysT_d = din("keysT", [2, 16, 128, 128])
    downP_d = din("downP", [2, 128, 128, 1024])
    upP_d = din("upP", [2, 128, 128, 1024])
    outT_d = nc.dram_tensor("outT", [nb, 8, 128, T], F32, kind="ExternalOutput")

    def dscr(name, shape, dt=F32):
        return nc.dram_tensor(name, list(shape), dt, kind="Internal")

    xs_d = [xT_d] + [nc.dram_tensor("xs%d" % i, [nb, 8, 128, T], F32, kind=("ExternalOutput" if dbg else "Internal"))
                     for i in range(1, 5)]
    xs_b = [[[Buf(None, "x%d_%d_%d" % (i, b, g)) for g in range(8)] for b in range(nb)] for i in range(5)]
    downB_d = dscr("downB", [2, 128, 128, 1024], BF16)
    upB_d = dscr("upB", [2, 128, 128, 1024], BF16)
    wqB_d = dscr("wqB", [2, 16, 128, 1024], BF16)
    wq_b = [Buf(None, "wqB%d" % l) for l in range(2)]
    tab_b = [[Buf(None, "tab%d_%d" % (l, q)) for q in range(16)] for l in range(2)]
    out_b = [Buf(None, "out%d" % b) for b in range(nb)]
    dbg_outs = {}

    def dbg_dump(name, src_ap, shape, dt, Rb):
        if not dbg:
            return
        d = nc.dram_tensor("dbg_" + name, list(shape), dt, kind="ExternalOutput")
        b = Buf(None, "dbg_" + name)
        S.dma("sp", d[tuple(slice(None) for _ in shape)], src_ap, R=Rb, W=[b])
        dbg_outs[name] = b

    ident = S.sbuf("ident", [128, 128], F32)
    identb = S.sbuf("identb", [128, 128], BF16)
    ones = S.sbuf("ones", [128, 128], F32)
    iot = S.sbuf("iot", [128, 128], F32)
    pidx = S.sbuf("pidx", [128, 1], F32)
    S.op("pool", lambda e: e.iota(iot[:, :], pattern=[[1, 128]], base=0, channel_multiplier=0,
                                  allow_small_or_imprecise_dtypes=True), W=[iot])
    S.op("pool", lambda e: e.iota(pidx[:, :], pattern=[[0, 1]], base=0, channel_multiplier=1,
                                  allow_small_or_imprecise_dtypes=True), W=[pidx])
    S.op("dve", lambda e: e.tensor_scalar(out=ident[:, :], in0=iot[:, :], scalar1=pidx[:, 0:1], scalar2=None,
                                          op0=ALU.is_equal), R=[iot, pidx], W=[ident])
    S.op("dve", lambda e: e.tensor_copy(out=identb[:, :], in_=ident[:, :]), R=[ident], W=[identb])
    S.op("dve", lambda e: e.memset(ones[:, :], 1.0), W=[ones])
    iotb = S.sbuf("iotb", [128, 128], BF16)
    S.op("dve", lambda e: e.tensor_copy(out=iotb[:, :], in_=iot[:, :]), R=[iot], W=[iotb])

    modT = S.sbuf("modT", [128, 2, 48, nb + 1], F32)
    A1 = S.sbuf("A1", [128, 2, 8, nb + 1], F32)
    A2 = S.sbuf("A2", [128, 2, 8, nb + 1], F32)
    n1g = S.sbuf("n1g_s", [128, 2, 8], F32)
    n2g = S.sbuf("n2g_s", [128, 2, 8], F32)
    fgs = S.sbuf("fg_s", [128, 8], F32)
    zero8 = S.sbuf("zero8", [128, 8], F32)
    cvw = S.sbuf("cvw_s", [128, 3, 8], F32)
    S.dma("sp", n1g[:, :, :], n1g_d[:, :, :], W=[n1g])
    S.dma("sp", n2g[:, :, :], n2g_d[:, :, :], W=[n2g])
    S.dma("sp", fgs[:, :], fg_d[:, :], W=[fgs])
    S.dma("sp", cvw[:, :, :], cvw_d[:, :, :], W=[cvw])
    S.op("dve", lambda e: e.memset(zero8[:, :], 0.0), W=[zero8])

    def conv_wq(l):
        for c in range(16):
            S.dma("pool", wqB_d[l, c].rearrange("p (k n) -> p k n", k=8),
                  wq_d[l, :, c * 128:(c + 1) * 128].rearrange("(k p) n -> p k n", p=128), W=[wq_b[l]], own_sem=32 + c)

    def conv_tab_q(l, q):
        S.dma("pool", downB_d[l, q * 8:(q + 1) * 8], downP_d[l, q * 8:(q + 1) * 8], W=[tab_b[l][q]], own_sem=2 * q)
        S.dma("pool", upB_d[l, q * 8:(q + 1) * 8], upP_d[l, q * 8:(q + 1) * 8], W=[tab_b[l][q]], own_sem=2 * q + 1)

    def conv_tables(l):
        conv_wq(l)
        if l == 0 and "l0" in stages:
            return
        for q in range(16):
            conv_tab_q(l, q)

    def conv0_gen():
        cin = [S.sbuf("cv_in%d" % i, [128, 2, 1024], F32) for i in range(3)]
        cout = [S.sbuf("cv_out%d" % i, [128, 2, 1024], BF16) for i in range(2)]
        tiles = [(src, dst, c0) for c0 in range(0, 128, 2) for (src, dst) in ((downP_d, downB_d), (upP_d, upB_d))]

        def ld(i):
            src, dst, c0 = tiles[i]
            S.dma("sp", cin[i % 3][:, :, :], src[0, c0:c0 + 2].rearrange("c p x -> p c x"), W=[cin[i % 3]])

        ld(0)
        ld(1)
        for i, (src, dst, c0) in enumerate(tiles):
            if i + 2 < len(tiles):
                ld(i + 2)
            ci, co = cin[i % 3], cout[i % 2]
            S.cp("act" if i % 2 else "dve", co[:, :, :], ci[:, :, :], R=[ci], W=[co])
            S.dma("sp", dst[0, c0:c0 + 2].rearrange("c p x -> p c x"), co[:, :, :], R=[co], W=[tab_b[0][c0 // 8]])
            yield

    if "conv" in stages:
        conv_tables(0)

    if "ada" in stages:
        with S.scope():
            ncol = nb + 1
            cTs = S.sbuf("cTs", [128, 8, ncol], F32)
            sil = S.sbuf("sil", [128, 8, ncol], F32)
            adab = S.sbuf("adab", [128, 2, 48], F32)
            S.dma("sp", cTs[:, :, :], cT_d[:, :, :], W=[cTs])
            S.dma("sp", adab[:, :, :], ada_b_d[:, :, :], W=[adab])
            S.op("act", lambda e: e.activation(out=sil[:, :, :], in_=cTs[:, :, :], func=AF.Silu), R=[cTs], W=[sil])
            wts = [S.sbuf("adaw%d" % i, [128, 8, 1024], F32) for i in range(2)]
            pss = [S.psum("adaps%d" % i, [128, 512], F32) for i in range(2)]
            pst = [S.psum("adapt%d" % i, [128, 512], F32) for i in range(2)]
            rows = [S.sbuf("adarow%d" % i, [ncol, 512], F32) for i in range(2)]
            it = 0
            for l in range(2):
                for part in range(6):
                    wt = wts[(l * 6 + part) % 2]
                    S.dma("sp", wt[:, :, :],
                          ada_w_d[l, :, part * 1024:(part + 1) * 1024].rearrange("(k p) n -> p k n", p=128), W=[wt])
                    for hb in range(2):
                        ps, row, pt = pss[it % 2], rows[it % 2], pst[it % 2]
                        it += 1
                        for k in range(8):
                            S.op("pe", lambda e, k=k: e.matmul(ps[0:ncol, :], lhsT=sil[:, k, :],
                                                               rhs=wt[:, k, hb * 512:(hb + 1) * 512],
                                                               start=(k == 0), stop=(k == 7)), R=[wt, sil], W=[ps], sig=(k == 7))
                        S.op("act", lambda e: e.copy(out=row[:, :], in_=ps[0:ncol, :]), R=[ps], W=[row])
                        for q4 in range(4):
                            S.op("pe", lambda e, q4=q4: e.transpose(pt[:, q4 * ncol:(q4 + 1) * ncol],
                                                                    row[:, q4 * 128:(q4 + 1) * 128], ident[0:ncol, 0:ncol]),
                                 R=[row, ident], W=[pt], sig=(q4 == 3))
                        col0 = part * 8 + hb * 4
                        S.op("dve", lambda e, col0=col0, l=l: e.tensor_tensor(
                            out=modT[:, l, col0:col0 + 4, :], in0=pt[:, 0:4 * ncol].rearrange("p (c n) -> p c n", c=4),
                            in1=adab[:, l, col0:col0 + 4].unsqueeze(2).to_broadcast([128, 4, ncol]), op=ALU.add),
                             R=[pt, adab], W=[modT])
            for l in range(2):
                S.op("dve", lambda e, l=l: e.scalar_tensor_tensor(
                    out=A1[:, l, :, :], in0=modT[:, l, 8:16, :], scalar=1.0,
                    in1=n1g[:, l, :].unsqueeze(2).to_broadcast([128, 8, ncol]),
                    op0=ALU.add, op1=ALU.mult), R=[modT, n1g], W=[A1])
                S.op("dve", lambda e, l=l: e.scalar_tensor_tensor(
                    out=A2[:, l, :, :], in0=modT[:, l, 32:40, :], scalar=1.0,
                    in1=n2g[:, l, :].unsqueeze(2).to_broadcast([128, 8, ncol]),
                    op0=ALU.add, op1=ALU.mult), R=[modT, n2g], W=[A2])
            dbg_dump("modT", modT[:, :, :, :], [128, 2, 48, nb + 1], F32, [modT])

    def norm_mod(src_ap_fn, ntok, Rsrc, A_ap, B_ap, dst, dst_is_dram=False, dstW=None, blk=256, tagn="nm"):
        with S.scope():
            xb = [S.sbuf(tagn + "_xb%d" % i, [128, 8, blk], F32) for i in range(2)]
            sq = S.sbuf(tagn + "_sq", [128, 8, blk], F32)
            rs = S.sbuf(tagn + "_rs", [128, blk], F32)
            ps = S.psum(tagn + "_ps", [128, 512], F32)
            ob = [S.sbuf(tagn + "_ob%d" % i, [128, 8, blk], F32) for i in range(2)] if dst_is_dram else None
            for n in range(ntok // blk):
                x_ = xb[n % 2]
                S.dma("sp", x_[:, :, :], src_ap_fn(n * blk, blk), R=Rsrc, W=[x_])
                S.op("act", lambda e, x_=x_: e.activation(out=sq[:, :, :], in_=x_[:, :, :], func=AF.Square),
                     R=[x_], W=[sq])
                for j in range(8):
                    S.op("pe", lambda e, j=j: e.matmul(ps[:, 0:blk], lhsT=ones[:, :], rhs=sq[:, j, :],
                                                       start=(j == 0), stop=(j == 7)),
                         R=[ones, sq], W=[ps], sig=(j == 7))
                S.op("act", lambda e: e.activation(out=rs[:, :], in_=ps[:, 0:blk], func=AF.Sqrt, bias=EPS,
                                                   scale=1.0 / D), R=[ps], W=[rs])
                S.op("dve", lambda e: e.reciprocal(out=rs[:, :], in_=rs[:, :]), R=[rs], W=[rs])
                S.op("dve", lambda e, x_=x_: e.tensor_tensor(
                    out=sq[:, :, :], in0=x_[:, :, :], in1=rs[:, :].unsqueeze(1).to_broadcast([128, 8, blk]),
                    op=ALU.mult), R=[x_, rs], W=[sq])
                S.op("pool", lambda e: e.tensor_tensor(
                    out=sq[:, :, :], in0=sq[:, :, :], in1=A_ap.unsqueeze(2).to_broadcast([128, 8, blk]),
                    op=ALU.mult), R=[sq, A1, A2, fgs], W=[sq])
                if dst_is_dram:
                    o_ = ob[n % 2]
                    S.op("dve", lambda e, o_=o_: e.tensor_tensor(
                        out=o_[:, :, :], in0=sq[:, :, :], in1=B_ap.unsqueeze(2).to_broadcast([128, 8, blk]),
                        op=ALU.add), R=[sq, modT, zero8], W=[o_])
                    S.dma("sp", dst(n * blk, blk), o_[:, :, :], R=[o_], W=dstW)
                else:
                    S.op("dve", lambda e, n=n: e.tensor_tensor(
                        out=dst[:, :, n * blk:(n + 1) * blk], in0=sq[:, :, :],
                        in1=B_ap.unsqueeze(2).to_broadcast([128, 8, blk]),
                        op=ALU.add), R=[sq, modT, zero8], W=[dst])

    def xsrc(i, b):
        return lambda t0, n: xs_d[i][b, :, :, t0:t0 + n].rearrange("j p t -> p j t")

    def grp_bufs(i, b, t0=0, n=T):
        return [xs_b[i][b][g] for g in range(t0 // 256, (t0 + n + 255) // 256)]

    def proj_fm(w_bf, col0, ncols_chunks, hT, ntok, evac, pss, tagc=[0]):
        nblk = (ntok + 511) // 512
        for m in range(ncols_chunks):
            for n in range(nblk):
                w = min(512, ntok - n * 512)
                ps = pss[tagc[0] % len(pss)]
                tagc[0] += 1
                for k in range(8):
                    S.op("pe", lambda e, k=k, m=m, n=n, w=w, ps=ps: e.matmul(
                        ps[:, 0:w], lhsT=w_bf[:, k, col0 + m * 128:col0 + (m + 1) * 128],
                        rhs=hT[:, k, n * 512:n * 512 + w], start=(k == 0), stop=(k == 7)),
                         R=[w_bf, hT], W=[ps], sig=(k == 7))
                evac(m, n, w, ps)

    def out_proj(w_d, srcT, l, b, xin, xout, tagn):
        with S.scope():
            w_bf = S.sbuf(tagn + "_w", [128, 8, D], BF16)
            S.dma("pool", w_bf[:, :, :], w_d.rearrange("(k p) n -> p k n", p=128), W=[w_bf])
            pss = [S.psum(tagn + "_ps%d" % i, [128, 512], F32) for i in range(2)]
            xbs = [S.sbuf(tagn + "_xb%d" % i, [128, 512], F32) for i in range(3)]
            cnt = [0]

            def evac(m, n, w, ps):
                xb = xbs[cnt[0] % 3]
                cnt[0] += 1
                S.dma("sp", xb[:, :], xs_d[xin][b, m, :, n * 512:(n + 1) * 512], R=grp_bufs(xin, b, n * 512, 512), W=[xb])
                S.op("dve", lambda e: e.scalar_tensor_tensor(
                    out=xb[:, :], in0=ps[:, :], scalar=modT[:, l, 16 + m, b:b + 1], in1=xb[:, :],
                    op0=ALU.mult, op1=ALU.add), R=[ps, modT, xb], W=[xb])
                S.dma("sp", xs_d[xout][b, m, :, n * 512:(n + 1) * 512], xb[:, :], R=[xb], W=grp_bufs(xout, b, n * 512, 512))

            proj_fm(w_bf, 0, 8, srcT, T, evac, pss)

    def layer0(b):
        with S.scope():
            catT = S.sbuf("catT", [128, 8, T], BF16)
            with S.scope():
                FT = S.sbuf("FT", [128, 4, T], BF16)
                with S.scope():
                    QT = S.sbuf("QT", [128, 4, T], BF16)
                    KT = S.sbuf("KT", [128, 4, T], BF16)
                    V = S.sbuf("V", [128, 16, 8, 65], BF16)
                    KcT = S.sbuf("KcT", [128, 4, CTX], BF16)
                    Vc = S.sbuf("Vc", [128, 2, 8, 65], BF16)
                    S.op("pool", lambda e: e.memset(V[:, :, :, 64:65], 1.0), W=[V])
                    S.op("pool", lambda e: e.memset(Vc[:, :, :, 64:65], 1.0), W=[Vc])
                    with S.scope():
                        hT = S.sbuf("hT", [128, 8, T], BF16)
                        hcT = S.sbuf("hcT", [128, 8, CTX], BF16)
                        norm_mod(xsrc(0, b), T, [], A1[:, 0, :, b], modT[:, 0, 0:8, b], hT, tagn="n0")
                        norm_mod(lambda t0, n: ctxT_d[b, :, :, t0:t0 + n].rearrange("j p t -> p j t"), CTX, [],
                                 A1[:, 0, :, nb], modT[:, 0, 0:8, nb], hcT, tagn="n0c")
                        if b == 0:
                            dbg_dump("hT", hT[:, :, :], [128, 8, T], BF16, [hT])
                        with S.scope():
                            wi = S.sbuf("wi", [128, 8, 2048], BF16)
                            S.dma("pool", wi[:, :, :], wi_d.rearrange("(k p) n -> p k n", p=128), W=[wi])
                            pss = [S.psum("pj_ps%d" % i, [128, 512], F32) for i in range(3)]
                            ec = [0]

                            def ev_to(dst, scale=None):
                                def evac(m, n, w, ps):
                                    ec[0] += 1
                                    o = dst[:, m, n * 512:n * 512 + w]
                                    if scale is not None:
                                        S.op("act", lambda e: e.activation(out=o, in_=ps[:, 0:w], func=AF.Identity,
                                                                           scale=scale), R=[ps], W=[dst])
                                    elif ec[0] % 2:
                                        S.op("act", lambda e: e.copy(out=o, in_=ps[:, 0:w]), R=[ps], W=[dst])
                                    else:
                                        S.op("dve", lambda e: e.tensor_copy(out=o, in_=ps[:, 0:w]), R=[ps], W=[dst])
                                return evac

                            proj_fm(wi, 0, 4, hT, T, ev_to(QT, 0.125), pss)
                            proj_fm(wi, 512, 4, hT, T, ev_to(KT), pss)
                            proj_fm(wi, 1536, 4, hT, T, ev_to(FT), pss)
                            proj_fm(wi, 512, 4, hcT, CTX, ev_to(KcT), pss)
                            for tt in range(16 + 2):
                                ps = pss[tt % 3]
                                src, c0, dstV, di = (hT, tt * 128, V, tt) if tt < 16 else (hcT, (tt - 16) * 128, Vc, tt - 16)
                                for k in range(8):
                                    S.op("pe", lambda e, k=k, ps=ps, src=src, c0=c0: e.matmul(
                                        ps[:, :], lhsT=src[:, k, c0:c0 + 128], rhs=wi[:, k, 1024:1536],
                                        start=(k == 0), stop=(k == 7)), R=[src, wi], W=[ps], sig=(k == 7))
                                S.cp("act" if tt % 2 else "dve", dstV[:, di, :, 0:64], ps[:, :].rearrange("p (h d) -> p h d", h=8), R=[ps], W=[dstV])
                    if b == 0:
                        dbg_dump("QT", QT[:, :, :], [128, 4, T], BF16, [QT])
                        dbg_dump("V", V[:, :, :, :], [128, 16, 8, 65], BF16, [V])
                    with S.scope():
                        a_tok = S.sbuf("a_tok", [128, 16, 512], BF16)
                        bxs = [S.sbuf("bx%d" % i, [128, 5, 640], F32) for i in range(2)]
                        tmps = [S.sbuf("atmp%d" % i, [128, 640], F32) for i in range(2)]
                        PTs = [S.sbuf("PT%d" % i, [128, 896], BF16) for i in range(2)]
                        rden = [S.sbuf("rden%d" % i, [128, 1], F32) for i in range(2)]
                        ps_s = [[S.psum("ps_s%d_%d" % (i, k), [128, 512], F32) for k in range(2)] for i in range(2)]
                        ps_o = [S.psum("ps_o%d" % i, [128, 512], F32) for i in range(2)]
                        u = 0
                        cgen = conv0_gen() if (b == 0 and "conv" in stages) else None
                        units = [(h, j) for h in range(8) for j in range(16)]
                        bx_of = {}

                        def stageA(u):
                            h, j = units[u]
                            hc, r0 = h // 2, (h % 2) * 64
                            if j == 0:
                                bx_ = bxs[h % 2]
                                S.dma("sp", bx_[:, :, :], biasx_d[h].rearrange("p (c q) -> p c q", c=5), W=[bx_])
                                bx_of[h] = bx_
                            bx = bx_of[h]
                            kp0, cls = _att_kp0(j), _att_class(j)
                            pA, pB = ps_s[u % 2]
                            tmp, PT = tmps[u % 2], PTs[u % 2]
                            q_ap = QT[r0:r0 + 64, hc, j * 128:(j + 1) * 128]
                            for p in range(4):
                                S.op("pe", lambda e, p=p: e.matmul(
                                    pA[:, p * 128:(p + 1) * 128],
                                    lhsT=KT[r0:r0 + 64, hc, (kp0 + p) * 128:(kp0 + p + 1) * 128], rhs=q_ap,
                                    start=True, stop=True), R=[KT, QT], W=[pA], sig=(p == 3))
                            S.op("pe", lambda e: e.matmul(
                                pB[:, 0:128], lhsT=KT[r0:r0 + 64, hc, (kp0 + 4) * 128:(kp0 + 5) * 128], rhs=q_ap,
                                start=True, stop=True), R=[KT, QT], W=[pB], sig=False)
                            for lt in range(2):
                                S.op("pe", lambda e, lt=lt: e.matmul(
                                    pB[:, 128 + lt * 128:256 + lt * 128],
                                    lhsT=KcT[r0:r0 + 64, hc, lt * 128:(lt + 1) * 128], rhs=q_ap,
                                    start=True, stop=True), R=[KcT, QT], W=[pB], sig=(lt == 1))
                            S.op("dve", lambda e: e.tensor_tensor(out=tmp[:, 0:512], in0=pA[:, :], in1=bx[:, cls, 0:512],
                                                                  op=ALU.add), R=[pA, bx], W=[tmp])
                            S.op("dve", lambda e: e.tensor_tensor(out=tmp[:, 512:640], in0=pB[:, 0:128],
                                                                  in1=bx[:, cls, 512:640], op=ALU.add),
                                 R=[pB, bx], W=[tmp])
                            S.op("act", lambda e: e.activation(out=PT[:, 0:640], in_=tmp[:, :], func=AF.Exp),
                                 R=[tmp], W=[PT])
                            S.op("act", lambda e: e.activation(out=PT[:, 640:896], in_=pB[:, 128:384], func=AF.Exp),
                                 R=[pB], W=[PT])

                        def stageB(u):
                            h, j = units[u]
                            kp0 = _att_kp0(j)
                            PT, rd, po = PTs[u % 2], rden[u % 2], ps_o[u % 2]
                            for p in range(7):
                                rhs = V[:, kp0 + p, h, :] if p < 5 else Vc[:, p - 5, h, :]
                                S.op("pe", lambda e, p=p, rhs=rhs: e.matmul(
                                    po[:, 0:65], lhsT=PT[:, p * 128:(p + 1) * 128], rhs=rhs,
                                    start=(p == 0), stop=(p == 6)), R=[PT, V, Vc], W=[po], sig=(p == 6))
                            S.op("dve", lambda e: e.reciprocal(out=rd[:, :], in_=po[:, 64:65]), R=[po], W=[rd])
                            S.op("dve", lambda e: e.tensor_scalar(
                                out=a_tok[:, j, h * 64:(h + 1) * 64], in0=po[:, 0:64], scalar1=rd[:, 0:1],
                                scalar2=None, op0=ALU.mult), R=[po, rd], W=[a_tok])

                        stageA(0)
                        for u in range(len(units)):
                            if u + 1 < len(units):
                                stageA(u + 1)
                            stageB(u)
                            if cgen is not None:
                                next(cgen, None)
                        if cgen is not None:
                            for _ in cgen:
                                pass
                        pts = [S.psum("a_pt%d" % i, [128, 8, 128], BF16) for i in range(2)]
                        for tt in range(16):
                            pt = pts[tt % 2]
                            for m in range(4):
                                S.op("pe", lambda e, m=m: e.transpose(pt[:, m, :], a_tok[:, tt, m * 128:(m + 1) * 128],
                                                                      identb[:, :]),
                                     R=[a_tok, identb], W=[pt], sig=(m == 3))
                            S.cp("act" if tt % 2 else "dve", catT[:, 0:4, tt * 128:(tt + 1) * 128], pt[:, 0:4, :], R=[pt], W=[catT])
                with S.scope():
                    U_sb = S.sbuf("U_sb", [128, 16, 512], BF16)
                    W_sb = S.sbuf("W_sb", [128, 16, 512], BF16)
                    bdc = S.sbuf("bdc", [128, 128], BF16)
                    bds = S.sbuf("bds", [128, 128], BF16)
                    bdw = S.sbuf("bdw", [128, 4, 128], BF16)
                    S.dma("sp", bdc[:, :], bdc_d[:, :], W=[bdc])
                    S.dma("sp", bds[:, :], bds_d[:, :], W=[bds])
                    S.op("dve", lambda e: e.memset(bdw[:, :, :], 0.0), W=[bdw])
                    for g in range(8):
                        r0 = (g % 2) * 64
                        S.dma("pool", bdw[r0:r0 + 64, g // 2, r0:r0 + 64], fnw_d[g], W=[bdw])
                    pss = [S.psum("fn_ps%d" % i, [128, 512], F32) for i in range(4)]
                    for tt in range(16):
                        pu, pw = pss[(tt % 2) * 2], pss[(tt % 2) * 2 + 1]
                        for m in range(4):
                            S.op("pe", lambda e, m=m: e.matmul(pu[:, m * 128:(m + 1) * 128],
                                                               lhsT=FT[:, m, tt * 128:(tt + 1) * 128], rhs=bdc[:, :],
                                                               start=True, stop=True), R=[FT, bdc], W=[pu], sig=(m == 3))
                        for m in range(4):
                            S.op("pe", lambda e, m=m: e.matmul(pw[:, m * 128:(m + 1) * 128],
                                                               lhsT=FT[:, m, tt * 128:(tt + 1) * 128], rhs=bds[:, :],
                                                               start=True, stop=True), R=[FT, bds], W=[pw], sig=(m == 3))
                        S.op("act", lambda e: e.copy(out=U_sb[:, tt, :], in_=pu[:, :]), R=[pu], W=[U_sb])
                        S.op("dve", lambda e: e.tensor_copy(out=W_sb[:, tt, :], in_=pw[:, :]), R=[pw], W=[W_sb])
                    CNs = [S.sbuf("CN%d" % i, [128, 16, 512], BF16) for i in range(2)]
                    SNs = [S.sbuf("SN%d" % i, [128, 16, 512], BF16) for i in range(2)]
                    spT = [S.sbuf("spT%d" % i, [128, 512], BF16) for i in range(2)]
                    it = 0
                    for nbk in range(4):
                        CN, SN = CNs[nbk % 2], SNs[nbk % 2]
                        S.dma("sp", CN[:, :, :], dftc_d[:, nbk * 512:(nbk + 1) * 512].rearrange("(t p) n -> p t n", p=128), W=[CN])
                        S.dma("sp", SN[:, :, :], dfts_d[:, nbk * 512:(nbk + 1) * 512].rearrange("(t p) n -> p t n", p=128), W=[SN])
                        for m in range(4):
                            ps, ps2, sp = pss[it % 2], pss[2 + it % 2], spT[it % 2]
                            it += 1
                            for tt in range(16):
                                S.op("pe", lambda e, tt=tt: e.matmul(ps[:, :], lhsT=U_sb[:, tt, m * 128:(m + 1) * 128],
                                                                     rhs=CN[:, tt, :], start=(tt == 0), stop=False),
                                     R=[U_sb, CN], W=[ps], sig=False)
                                S.op("pe", lambda e, tt=tt: e.matmul(ps[:, :], lhsT=W_sb[:, tt, m * 128:(m + 1) * 128],
                                                                     rhs=SN[:, tt, :], start=False, stop=(tt == 15)),
                                     R=[W_sb, SN], W=[ps], sig=(tt == 15))
                            S.op("act", lambda e: e.activation(out=sp[:, :], in_=ps[:, :], func=AF.Identity,
                                                               scale=float(1.0 / np.sqrt(T * 64.0))), R=[ps], W=[sp])
                            S.op("pe", lambda e: e.matmul(ps2[:, :], lhsT=bdw[:, m, :], rhs=sp[:, :], start=True, stop=True),
                                 R=[bdw, sp], W=[ps2])
                            S.op("dve", lambda e: e.tensor_copy(out=catT[:, 4 + m, nbk * 512:(nbk + 1) * 512], in_=ps2[:, :]),
                                 R=[ps2], W=[catT])
            if b == 0:
                dbg_dump("catT", catT[:, :, :], [128, 8, T], BF16, [catT])
            out_proj(wo_d, catT, 0, b, 0, 1, "o0")

    def layer1(b):
        with S.scope():
            zT = S.sbuf("zT", [128, 8, T], BF16)
            with S.scope():
                hT = S.sbuf("hT1", [128, 8, T], BF16)
                norm_mod(xsrc(2, b), T, grp_bufs(2, b), A1[:, 1, :, b], modT[:, 1, 0:8, b], hT, tagn="n1")
                with S.scope():
                    wci = S.sbuf("wci", [128, 8, 3 * D], BF16)
                    for q in range(3):
                        S.dma("pool", wci[:, :, q * D:(q + 1) * D],
                              cwi_d[:, q * D:(q + 1) * D].rearrange("(k p) n -> p k n", p=128), W=[wci])
                    us = [S.sbuf("u%d" % i, [128, T + 2], F32) for i in range(2)]
                    bgs = [S.sbuf("bg%d" % i, [128, T], F32) for i in range(2)]
                    ys = [S.sbuf("y%d" % i, [128, T], F32) for i in range(2)]
                    tcs = [S.sbuf("tc%d" % i, [128, 512], F32) for i in range(2)]
                    pss = [S.psum("cv_ps%d" % i, [128, 512], F32) for i in range(6)]
                    for i in range(2):
                        S.op("dve", lambda e, i=i: e.memset(us[i][:, :], 0.0), W=[us[i]])
                    cnt = 0
                    for m in range(8):
                        u_, bg, y_ = us[m % 2], bgs[m % 2], ys[m % 2]
                        for n in range(4):
                            pb, pc, pv = pss[(cnt % 2) * 3], pss[(cnt % 2) * 3 + 1], pss[(cnt % 2) * 3 + 2]
                            tc_ = tcs[cnt % 2]
                            cnt += 1
                            for (ps, c0) in ((pb, 0), (pc, D), (pv, 2 * D)):
                                for k in range(8):
                                    S.op("pe", lambda e, k=k, ps=ps, c0=c0: e.matmul(
                                        ps[:, :], lhsT=wci[:, k, c0 + m * 128:c0 + (m + 1) * 128],
                                        rhs=hT[:, k, n * 512:(n + 1) * 512], start=(k == 0), stop=(k == 7)),
                                         R=[wci, hT], W=[ps], sig=(k == 7))
                            S.op("act", lambda e: e.copy(out=bg[:, n * 512:(n + 1) * 512], in_=pb[:, :]), R=[pb], W=[bg])
                            S.op("act", lambda e: e.copy(out=tc_[:, :], in_=pc[:, :]), R=[pc], W=[tc_])
                            S.op("dve", lambda e: e.tensor_tensor(out=u_[:, 1 + n * 512:1 + (n + 1) * 512], in0=tc_[:, :],
                                                                  in1=pv[:, :], op=ALU.mult), R=[tc_, pv], W=[u_])
                        S.op("act", lambda e: e.activation(out=y_[:, :], in_=u_[:, 0:T], func=AF.Copy,
                                                           scale=cvw[:, 0, m:m + 1]), R=[u_, cvw], W=[y_])
                        S.op("dve", lambda e: e.scalar_tensor_tensor(out=y_[:, :], in0=u_[:, 1:T + 1],
                                                                     scalar=cvw[:, 1, m:m + 1], in1=y_[:, :],
                                                                     op0=ALU.mult, op1=ALU.add), R=[u_, cvw, y_], W=[y_])
                        S.op("dve", lambda e: e.scalar_tensor_tensor(out=y_[:, :], in0=u_[:, 2:T + 2],
                                                                     scalar=cvw[:, 2, m:m + 1], in1=y_[:, :],
                                                                     op0=ALU.mult, op1=ALU.add), R=[u_, cvw, y_], W=[y_])
                        S.op("dve", lambda e: e.tensor_tensor(out=zT[:, m, :], in0=y_[:, :], in1=bg[:, :], op=ALU.mult),
                             R=[y_, bg], W=[zT])
            out_proj(cwo_d, zT, 1, b, 2, 3, "o1")

    def peer(l, bs, xin, xout, ngroups):
        items = [(b_, g_) for b_ in bs for g_ in range(ngroups)]
        with S.scope():
            keysT = S.sbuf("keysT", [128, 16, 128], BF16)
            S.dma("pool", keysT[:, :, :], keysT_d[l].rearrange("c k n -> k c n"), W=[keysT])
            iot16 = iot[:, 0:16]
            xgs = [S.sbuf("xg%d" % i, [128, 8, 256], F32) for i in range(2)]
            h2s = [S.sbuf("h2%d" % i, [128, 8, 256], BF16) for i in range(2)]
            TRs = [S.sbuf("TR%d" % i, [128, 2, 3, 128], F32) for i in range(2)]
            scrA = S.sbuf("scrA", [128, 2048], F32)
            scrB = S.sbuf("scrB", [128, 2048], F32)
            rs = S.sbuf("p_rs", [128, 256], F32)
            qT = S.sbuf("qT", [128, 16, 256], BF16)
            wqs = [S.sbuf("wq%d" % i, [128, 8, 128], BF16) for i in range(3)]
            v1 = S.sbuf("v1", [128, 16, 16], F32)
            i1 = S.sbuf("i1", [128, 16, 16], U32)
            i1f = S.sbuf("i1f", [128, 16, 16], F32)
            wks = [S.sbuf("wk%d" % i, [128, 128], F32) for i in range(2)]
            wk2s = [S.sbuf("wk2%d" % i, [128, 256], F32) for i in range(2)]
            ts = S.sbuf("ts", [128, 8, 16], F32)
            pos = S.sbuf("pos", [128, 8, 16], U32)
            k12 = S.sbuf("k12", [128, 2, 128], U32)
            k12f = S.sbuf("k12f", [128, 2, 128], F32)
            ex = S.sbuf("ex", [128, 8, 16], F32)
            zs = S.sbuf("zs", [128, 8], F32)
            TT = S.sbuf("TT", [128, 3, 128], F32)
            NT = 16
            Aohs = [S.sbuf("Aoh%d" % i, [128, NT, 128], BF16) for i in range(2)]
            Bohs = [S.sbuf("Boh%d" % i, [128, NT, 128], BF16) for i in range(2)]
            ohc = [0]
            Aohs_p = [Buf(a.t, "AohP") for a in Aohs]
            Gsb = S.sbuf("Gsb", [128, 256, 128], BF16)
            dns = [S.sbuf("dn%d" % i, [128, 4, 1024], BF16) for i in range(2)]
            ups = [S.sbuf("up%d" % i, [128, 4, 1024], BF16) for i in range(2)]
            gls = [S.sbuf("gl%d" % i, [128, 256], F32) for i in range(2)]
            wEs = [S.sbuf("wE%d" % i, [128, 256], BF16) for i in range(2)]
            acc = [S.psum("acc%d" % i, [128, 512], F32) for i in range(4)]
            pa = [S.psum("pa%d" % i, [128, 512], F32) for i in range(2)]
            bkP = [S.psum("bkP%d" % i, [128, 512], F32) for i in range(2)]
            sq3 = scrA[:, :].rearrange("p (j t) -> p j t", j=8)
            eqt = scrA[:, :].rearrange("p (a i) -> p a i", i=16)
            s_sb = scrB
            cand3 = scrB[:, :].rearrange("p (h c) -> p h c", h=8)
            bkc = [0]
            v1_b = [Buf(v1.t, "v1a"), Buf(v1.t, "v1b")]
            i1_b = [Buf(i1.t, "i1a"), Buf(i1.t, "i1b")]
            ts_b = [Buf(ts.t, "tsa"), Buf(ts.t, "tsb")]
            pos_b = [Buf(pos.t, "posa"), Buf(pos.t, "posb")]

            def nbank():
                bkc[0] += 1
                return bkP[bkc[0] % 2]

            def prep1(it):
                b, g = items[it]
                slot = it % 2
                t0 = g * 256
                xg, h2, TR = xgs[slot], h2s[slot], TRs[slot]
                S.dma("sp", xg[:, :, :], xs_d[xin][b, :, :, t0:t0 + 256].rearrange("j p t -> p j t"),
                      R=[xs_b[xin][b][g]], W=[xg])
                yield
                S.op("act", lambda e: e.activation(out=sq3, in_=xg[:, :, :], func=AF.Square), R=[xg], W=[scrA])
                bn = nbank()
                for j in range(8):
                    S.op("pe", lambda e, j=j: e.matmul(bn[:, 0:256], lhsT=ones[:, :], rhs=sq3[:, j, :],
                                                       start=(j == 0), stop=(j == 7)), R=[ones, scrA], W=[bn], sig=(j == 7))
                yield
                S.op("act", lambda e: e.activation(out=rs[:, :], in_=bn[:, 0:256], func=AF.Sqrt, bias=EPS, scale=1.0 / D),
                     R=[bn], W=[rs])
                S.op("dve", lambda e: e.reciprocal(out=rs[:, :], in_=rs[:, :]), R=[rs], W=[rs])
                S.op("dve", lambda e: e.tensor_tensor(out=sq3, in0=xg[:, :, :],
                                                      in1=rs[:, :].unsqueeze(1).to_broadcast([128, 8, 256]), op=ALU.mult),
                     R=[xg, rs], W=[scrA])
                yield
                yield
                S.op("dve", lambda e: e.tensor_tensor(out=sq3, in0=sq3,
                                                      in1=A2[:, l, :, b].unsqueeze(2).to_broadcast([128, 8, 256]),
                                                      op=ALU.mult), R=[scrA, A2], W=[scrA])
                yield
                S.op("dve", lambda e: e.tensor_tensor(out=h2[:, :, :], in0=sq3,
                                                      in1=modT[:, l, 24:32, b].unsqueeze(2).to_broadcast([128, 8, 256]),
                                                      op=ALU.add), R=[scrA, modT], W=[h2])
                yield
                yield
                yield
                prev = None

                def ldwq(c):
                    S.dma("sp", wqs[c % 3][:, :, :], wqB_d[l, c].rearrange("p (k n) -> p k n", k=8), R=[wq_b[l]],
                          W=[wqs[c % 3]])

                ldwq(0)
                ldwq(1)
                for c in range(16):
                    wq = wqs[c % 3]
                    if c + 2 < 16:
                        ldwq(c + 2)
                    ps = nbank()
                    for k in range(8):
                        S.op("pe", lambda e, k=k: e.matmul(ps[:, 0:256], lhsT=wq[:, k, :], rhs=h2[:, k, :],
                                                           start=(k == 0), stop=(k == 7)), R=[wq, h2], W=[ps], sig=(k == 7))
                    if prev is not None:
                        S.cp("act", qT[:, prev[0], :], prev[1][:, 0:256], R=[prev[1]], W=[qT])
                    prev = (c, ps)
                    yield
                S.cp("act", qT[:, prev[0], :], prev[1][:, 0:256], R=[prev[1]], W=[qT])
                yield
                for tl in range(2):
                    prev = None
                    for q4 in range(4):
                        bq = nbank()
                        for cc in range(4):
                            c = q4 * 4 + cc
                            S.op("pe", lambda e, c=c, cc=cc: e.matmul(bq[:, cc * 128:(cc + 1) * 128],
                                                                      lhsT=qT[:, c, tl * 128:(tl + 1) * 128], rhs=keysT[:, c, :],
                                                                      start=True, stop=True), R=[qT, keysT], W=[bq], sig=(cc == 3))
                        if prev is not None:
                            S.cp("act", s_sb[:, prev[0] * 512:(prev[0] + 1) * 512], prev[1][:, :], R=[prev[1]], W=[scrB])
                        prev = (q4, bq)
                        yield
                    S.cp("act", s_sb[:, prev[0] * 512:(prev[0] + 1) * 512], prev[1][:, :], R=[prev[1]], W=[scrB])
                    yield
                    for c0 in range(0, 16, 2):
                        cs = (c0, c0 + 1)
                        scs = [s_sb[:, c * 128:(c + 1) * 128] for c in cs]
                        for q_, c in enumerate(cs):
                            S.op("dve", lambda e, q_=q_, c=c: e.max(out=v1[:, c, 0:8], in_=scs[q_]), R=[scrB], W=[v1_b[q_]])
                        for q_, c in enumerate(cs):
                            S.op("dve", lambda e, q_=q_, c=c: e.max_index(out=i1[:, c, 0:8], in_max=v1[:, c, 0:8],
                                                                         in_values=scs[q_]), R=[scrB, v1_b[q_]], W=[i1_b[q_]])
                        for q_, c in enumerate(cs):
                            S.op("dve", lambda e, q_=q_, c=c: e.match_replace(out=wks[q_][:, :], in_to_replace=v1[:, c, 0:8],
                                                                             in_values=scs[q_], imm_value=-1e30),
                                 R=[scrB, v1_b[q_]], W=[wks[q_]])
                        yield
                        for q_, c in enumerate(cs):
                            S.op("dve", lambda e, q_=q_, c=c: e.max(out=v1[:, c, 8:16], in_=wks[q_][:, :]), R=[wks[q_]], W=[v1_b[q_]])
                        for q_, c in enumerate(cs):
                            S.op("dve", lambda e, q_=q_, c=c: e.max_index(out=i1[:, c, 8:16], in_max=v1[:, c, 8:16],
                                                                         in_values=wks[q_][:, :]), R=[wks[q_], v1_b[q_]], W=[i1_b[q_]])
                        yield
                    v1v = v1[:, :, :].rearrange("p (h two) k -> p h two k", two=2)
                    S.op("dve", lambda e: e.tensor_tensor(
                        out=cand3.rearrange("p h (i j) -> p h i j", i=16),
                        in0=v1v[:, :, 0, :].unsqueeze(3).to_broadcast([128, 8, 16, 16]),
                        in1=v1v[:, :, 1, :].unsqueeze(2).to_broadcast([128, 8, 16, 16]), op=ALU.add), R=v1_b, W=[scrB])
                    yield
                    for h0 in range(0, 8, 2):
                        hs = (h0, h0 + 1)
                        chs = [cand3[:, h, :] for h in hs]
                        for q_, h in enumerate(hs):
                            S.op("dve", lambda e, q_=q_, h=h: e.max(out=ts[:, h, 0:8], in_=chs[q_]), R=[scrB], W=[ts_b[q_]])
                        for q_, h in enumerate(hs):
                            S.op("dve", lambda e, q_=q_, h=h: e.max_index(out=pos[:, h, 0:8], in_max=ts[:, h, 0:8],
                                                                         in_values=chs[q_]), R=[scrB, ts_b[q_]], W=[pos_b[q_]])
                        for q_, h in enumerate(hs):
                            S.op("dve", lambda e, q_=q_, h=h: e.match_replace(out=wk2s[q_][:, :], in_to_replace=ts[:, h, 0:8],
                                                                             in_values=chs[q_], imm_value=-1e30),
                                 R=[scrB, ts_b[q_]], W=[wk2s[q_]])
                        yield
                        for q_, h in enumerate(hs):
                            S.op("dve", lambda e, q_=q_, h=h: e.max(out=ts[:, h, 8:16], in_=wk2s[q_][:, :]), R=[wk2s[q_]], W=[ts_b[q_]])
                        for q_, h in enumerate(hs):
                            S.op("dve", lambda e, q_=q_, h=h: e.max_index(out=pos[:, h, 8:16], in_max=ts[:, h, 8:16],
                                                                         in_values=wk2s[q_][:, :]), R=[wk2s[q_], ts_b[q_]], W=[pos_b[q_]])
                        yield
                    posf = pos[:, :, :].rearrange("p h k -> p (h k)")
                    S.op("dve", lambda e: e.tensor_single_scalar(out=k12[:, 0, :], in_=posf, scalar=4,
                                                                 op=ALU.logical_shift_right), R=pos_b, W=[k12])
                    S.op("dve", lambda e: e.tensor_single_scalar(out=k12[:, 1, :], in_=posf, scalar=15,
                                                                 op=ALU.bitwise_and), R=pos_b, W=[k12])
                    S.op("dve", lambda e: e.tensor_copy(out=k12f[:, :, :], in_=k12[:, :, :]), R=[k12], W=[k12f])
                    S.op("dve", lambda e: e.tensor_copy(out=i1f[:, :, :], in_=i1[:, :, :]), R=i1_b, W=[i1f])
                    yield
                    i1v = i1f[:, :, :].rearrange("p (h two) k -> p h two k", two=2)
                    eq4 = scrA[:, :].rearrange("p (h k i) -> p h k i", h=8, k=16)
                    for w_ in range(2):
                        S.op("dve", lambda e, w_=w_: e.tensor_tensor(
                            out=eqt, in0=k12f[:, w_, :].unsqueeze(2).to_broadcast([128, 128, 16]),
                            in1=iot16.unsqueeze(1).to_broadcast([128, 128, 16]), op=ALU.is_equal), R=[k12f, iot], W=[scrA])
                        S.op("dve", lambda e, w_=w_: e.tensor_tensor(
                            out=eq4, in0=eq4, in1=i1v[:, :, w_, :].unsqueeze(2).to_broadcast([128, 8, 16, 16]),
                            op=ALU.mult), R=[scrA, i1f], W=[scrA])
                        S.op("dve", lambda e, w_=w_: e.tensor_reduce(out=TR[:, tl, w_, :], in_=eqt, axis=AX.X, op=ALU.add),
                             R=[scrA], W=[TR])
                        yield
                    S.op("dve", lambda e: e.tensor_tensor(out=ex[:, :, :], in0=ts[:, :, :],
                                                          in1=ts[:, :, 0:1].to_broadcast([128, 8, 16]), op=ALU.subtract),
                         R=ts_b, W=[ex])
                    yield
                    yield
                    S.op("act", lambda e: e.activation(out=ex[:, :, :], in_=ex[:, :, :], func=AF.Exp), R=[ex], W=[ex])
                    yield
                    S.op("dve", lambda e: e.tensor_reduce(out=zs[:, :], in_=ex[:, :, :], axis=AX.X, op=ALU.add), R=[ex], W=[zs])
                    S.op("dve", lambda e: e.reciprocal(out=zs[:, :], in_=zs[:, :]), R=[zs], W=[zs])
                    S.op("dve", lambda e: e.tensor_tensor(
                        out=TR[:, tl, 2, :].rearrange("p (h k) -> p h k", h=8), in0=ex[:, :, :],
                        in1=zs[:, :].unsqueeze(2).to_broadcast([128, 8, 16]), op=ALU.mult), R=[ex, zs], W=[TR])
                    yield

            def prep2(it):
                slot = it % 2
                TR = TRs[slot]
                for tl in range(2):
                    bt = nbank()
                    for w_ in range(3):
                        S.op("pe", lambda e, w_=w_: e.transpose(bt[:, w_ * 128:(w_ + 1) * 128], TR[:, tl, w_, :], ident[:, :]),
                             R=[TR, ident], W=[bt], sig=(w_ == 2))
                    S.op("act", lambda e: e.copy(out=TT[:, :, :].rearrange("p a t -> p (a t)"), in_=bt[:, 0:384]),
                         R=[bt], W=[TT])
                    for hf in range(128 // NT):
                        tsl = slice(hf * NT, (hf + 1) * NT)
                        Aoh, Boh, AohP = Aohs[ohc[0] % 2], Bohs[ohc[0] % 2], Aohs_p[ohc[0] % 2]
                        ohc[0] += 1
                        for t_ in range(NT):
                            col = hf * NT + t_
                            S.op("dve", lambda e, t_=t_, col=col: e.tensor_scalar(
                                out=Aoh[:, t_, :], in0=iotb[:, :], scalar1=TT[:, 0, col:col + 1], scalar2=TT[:, 2, col:col + 1],
                                op0=ALU.is_equal, op1=ALU.mult), R=[iotb, TT], W=[Aoh], sig=False)
                        S.op("dve", lambda e: e.tensor_tensor(
                            out=Boh[:, :, :], in0=iot[:, :].unsqueeze(1).to_broadcast([128, NT, 128]),
                            in1=TT[:, 1, tsl].unsqueeze(2).to_broadcast([128, NT, 128]), op=ALU.is_equal),
                             R=[iot, TT], W=[Boh])
                        for t4 in range(NT // 4):
                            pg = nbank()
                            for tq in range(4):
                                t_ = t4 * 4 + tq
                                S.op("pe", lambda e, t_=t_, tq=tq: e.matmul(
                                    pg[:, tq * 128:(tq + 1) * 128], lhsT=Aoh[:, t_, :], rhs=Boh[:, t_, :],
                                    start=True, stop=True), R=[Aoh, Boh], W=[pg], sig=(tq == 3))
                            tg = tl * 128 + hf * NT + t4 * 4
                            S.cp("act", Gsb[:, tg:tg + 4, :].rearrange("p t i -> p (t i)"), pg[:, :], R=[pg], W=[Gsb])

            def loop(it, nxt):
                b, g = items[it]
                slot = it % 2
                t0 = g * 256
                xg, h2 = xgs[slot], h2s[slot]

                def load(k):
                    S.dma("sp", dns[k % 2][:, :, :], downB_d[l, k * 4:(k + 1) * 4].rearrange("c p x -> p c x"),
                          R=[tab_b[l][k // 2]], W=[dns[k % 2]])
                    S.dma("sp", ups[k % 2][:, :, :], upB_d[l, k * 4:(k + 1) * 4].rearrange("c p x -> p c x"),
                          R=[tab_b[l][k // 2]], W=[ups[k % 2]])

                def down(c):
                    dn, ci = dns[(c // 4) % 2], c % 4
                    p_, gl, wE = pa[c % 2], gls[c % 2], wEs[c % 2]
                    for dk in range(8):
                        S.op("pe", lambda e, dk=dk: e.matmul(p_[:, 0:256], lhsT=dn[:, ci, dk * 128:(dk + 1) * 128],
                                                             rhs=h2[:, dk, :], start=(dk == 0), stop=(dk == 7)),
                             R=[dn, h2], W=[p_], sig=(dk == 7))
                    S.op("act", lambda e: e.activation(out=gl[:, :], in_=p_[:, 0:256], func=AF.Gelu), R=[p_], W=[gl])
                    S.op("dve", lambda e: e.tensor_tensor(out=wE[:, :], in0=gl[:, :], in1=Gsb[:, :, c], op=ALU.mult),
                         R=[gl, Gsb], W=[wE])

                def up_(c):
                    up, ci, wE = ups[(c // 4) % 2], c % 4, wEs[c % 2]
                    for dk in range(8):
                        a_ = acc[dk // 2]
                        S.op("pe", lambda e, dk=dk, a_=a_: e.matmul(
                            a_[:, (dk % 2) * 256:(dk % 2 + 1) * 256], lhsT=up[:, ci, dk * 128:(dk + 1) * 128], rhs=wE[:, :],
                            start=(c == 0 and dk % 2 == 0), stop=(c == 127 and dk % 2 == 1)), R=[up, wE], W=[a_],
                             sig=(dk == 7))

                load(0)
                load(1)
                down(0)
                for c in range(128):
                    if c + 1 < 128:
                        down(c + 1)
                    up_(c)
                    if c % 4 == 3 and c // 4 + 2 < 32:
                        load(c // 4 + 2)
                    if nxt is not None:
                        next(nxt, None)
                for dk in range(8):
                    a_ = acc[dk // 2]
                    S.op("dve", lambda e, dk=dk, a_=a_: e.scalar_tensor_tensor(
                        out=xg[:, dk, :], in0=a_[:, (dk % 2) * 256:(dk % 2 + 1) * 256],
                        scalar=modT[:, l, 40 + dk, b:b + 1], in1=xg[:, dk, :], op0=ALU.mult, op1=ALU.add),
                         R=[a_, modT, xg], W=[xg])
                S.dma("sp", xs_d[xout][b, :, :, t0:t0 + 256].rearrange("j p t -> p j t"), xg[:, :, :],
                      R=[xg], W=[xs_b[xout][b][g]])

            for _ in prep1(0):
                pass
            for it in range(len(items)):
                prep2(it)
                if l == 0 and "conv" in stages and it < 8:
                    if it == 0:
                        conv_wq(1)
                    for q in range(2 * it, 2 * it + 2):
                        conv_tab_q(1, q)
                nxt = prep1(it + 1) if it + 1 < len(items) else None
                loop(it, nxt)
                if nxt is not None:
                    for _ in nxt:
                        pass

    bs = list(range(nb))
    if "l0" in stages:
        for b in bs:
            layer0(b)
    if "conv" in stages and "peer0" not in stages:
        conv_tables(1)
    if "peer0" in stages:
        peer(0, bs, 1, 2, peer_groups)
    if "l1" in stages:
        for b in bs:
            layer1(b)
    if "peer1" in stages:
        peer(1, bs, 3, 4, peer_groups)
    if "final" in stages:
        for b in bs:
            norm_mod(xsrc(4, b), T, grp_bufs(4, b), fgs[:, :], zero8[:, :],
                     lambda t0, n, b=b: outT_d[b, :, :, t0:t0 + n].rearrange("j p t -> p j t"),
                     dst_is_dram=True, dstW=[out_b[b]], tagn="nf")
    S.finish(out_b + list(dbg_outs.values()) + [xb_ for i in range(5) for b_ in xs_b[i] for xb_ in b_], "sp")
    S.barrier()
    S.stacks[0].close()
    print("built: ops=%d waits=%d" % (S.n_ops, S.n_wait))
    return nc


def _fm(a):
    B, Tn, Dn = a.shape
    return np.ascontiguousarray(a.transpose(0, 2, 1).reshape(B, Dn // 128, 128, Tn))


def _pc(v):
    v = np.asarray(v)
    lead = v.shape[:-1]
    return np.ascontiguousarray(np.moveaxis(v.reshape(lead + (8, 128)), -1, 0))


def host_layout(inp, nb=NB, ncores=NCORES):
    cst = _consts()
    f32 = np.float32
    shared = {}
    shared["ada_w"] = np.ascontiguousarray(inp["ada_w"], f32)
    shared["ada_b"] = np.ascontiguousarray(inp["ada_b"].reshape(2, 48, 128).transpose(2, 0, 1), f32)
    shared["n1g"] = _pc(inp["norm1_g"]).astype(f32)
    shared["n2g"] = _pc(inp["norm2_g"]).astype(f32)
    shared["fg"] = _pc(inp["final_g"]).astype(f32)
    shared["ab_w_in"] = np.ascontiguousarray(inp["ab_w_in"][0], f32)
    shared["ab_w_out"] = np.ascontiguousarray(inp["ab_w_out"][0], f32)
    rpb = np.asarray(inp["na_rpb"][0], f32).reshape(8, 15 * 31)
    rpb_ext = np.concatenate([rpb, np.full((8, 1), NEG, f32)], axis=1)
    shared["biasx"] = np.ascontiguousarray(rpb_ext[:, cst["rpb_idx"]].reshape(8, 128, 5 * 640))
    shared["fn_w"] = np.ascontiguousarray(inp["fn_w"][0], f32)
    shared["dftc"] = cst["dftc"]
    shared["dfts"] = cst["dfts"]
    shared["bdc"] = cst["bdc"]
    shared["bds"] = cst["bds"]
    shared["cv_w_in"] = np.ascontiguousarray(inp["cv_w_in"][0], f32)
    shared["cvw"] = _pc(inp["cv_w"][0]).astype(f32)
    shared["cv_w_out"] = np.ascontiguousarray(inp["cv_w_out"][0], f32)
    shared["peer_w_q"] = np.ascontiguousarray(inp["peer_w_q"], f32)
    shared["keysT"] = np.ascontiguousarray(np.asarray(inp["peer_keys"], f32).reshape(2, 16, 128, 128).transpose(0, 1, 3, 2))
    dn = np.asarray(inp["peer_down"], f32).reshape(2, 128, 128, 8, 128)
    shared["downP"] = np.ascontiguousarray(dn.transpose(0, 2, 4, 3, 1)).reshape(2, 128, 128, 1024)
    up = np.asarray(inp["peer_up"], f32).reshape(2, 128, 128, 1024)
    shared["upP"] = np.ascontiguousarray(up.transpose(0, 2, 1, 3))
    xT = _fm(np.asarray(inp["x"], f32))
    ctxT = _fm(np.asarray(inp["ctx"], f32))
    c = np.asarray(inp["c"], f32)
    cc = np.asarray(inp["c_ctx"], f32)
    maps = []
    for i in range(ncores):
        m = dict(shared)
        m["xT"] = xT[i * nb:(i + 1) * nb]
        m["ctxT"] = ctxT[i * nb:(i + 1) * nb]
        cols = np.stack([c[i * nb + k] for k in range(nb)] + [cc], axis=-1)
        m["cT"] = np.ascontiguousarray(cols.reshape(8, 128, nb + 1).transpose(1, 0, 2))
        maps.append(m)
    return maps


def kernel(**inputs):
    maps = host_layout(inputs)
    nc = build_nc()
    res = run_bass_kernel_spmd(nc, maps, core_ids=list(range(NCORES)))
    outs = []
    for r in res.results:
        o = np.asarray(r["outT"])
        outs.append(o.reshape(NB, D, T).transpose(0, 2, 1))
    return np.ascontiguousarray(np.concatenate(outs, axis=0), dtype=np.float32)
```
